# Optimizing a Trainium2 kernel written in Bass

```python
import jax, jax.numpy as jnp
from jax import lax
import numpy as np

D_MODEL = 1024
BATCH = 2
SEQ = 16384
DEPTH = 4

CTX_LEN = 256
GRID_W = 64
N_MIXERS = 2
FFN_DIM = 2816
N_FOURIER_GROUPS = 4
FOURIER_GROUP_DIM = D_MODEL // N_FOURIER_GROUPS
RET_HEADS = 4
RET_QK_DIM = D_MODEL // RET_HEADS
RET_V_DIM = 2 * D_MODEL // RET_HEADS
RET_QK_TOTAL = RET_HEADS * RET_QK_DIM
RET_V_TOTAL = RET_HEADS * RET_V_DIM
RET_IN_DIM = 2 * RET_QK_TOTAL + 2 * RET_V_TOTAL
RET_CHUNK = 128
ROPE_BASE = 10000.0
N_MOD = 9
NORM_EPS = 1e-6
N_FOURIER_LAYERS = (DEPTH + N_MIXERS - 1) // N_MIXERS
N_RET_LAYERS = DEPTH // N_MIXERS

kernel_name = "hybrid_fourier_retention_prefix_dit"


def rmsnorm(x, g):
    xf = x.astype(jnp.float32)
    y = xf * lax.rsqrt(jnp.mean(xf * xf, axis=-1, keepdims=True) + NORM_EPS)
    return (y * g.astype(jnp.float32)).astype(x.dtype)


def modulate(h, shift, scale):
    return h * (1 + scale) + shift


def swiglu(h, w_gate, w_up, w_down):
    return (jax.nn.silu(h @ w_gate) * (h @ w_up)) @ w_down


def fourier_mix(h):
    b, n, d = h.shape
    hg = h.reshape(b, n, N_FOURIER_GROUPS, FOURIER_GROUP_DIM).astype(jnp.float32)
    f = jnp.fft.fft2(hg, axes=(1, 3), norm="ortho").real
    return f.reshape(b, n, d).astype(h.dtype)


def fourier_layer(h_lat, h_ctx, w, bias):
    y_lat = fourier_mix(h_lat) @ w + bias
    y_ctx = None if h_ctx is None else fourier_mix(h_ctx) @ w + bias
    return y_lat, y_ctx


def axial_rope(n_tokens):
    rows = n_tokens // GRID_W
    row = jnp.repeat(jnp.arange(rows, dtype=jnp.float32), GRID_W)
    col = jnp.tile(jnp.arange(GRID_W, dtype=jnp.float32), rows)
    n_freq = RET_QK_DIM // 4
    inv_freq = ROPE_BASE ** (-jnp.arange(n_freq, dtype=jnp.float32) / n_freq)
    ang = jnp.concatenate([row[:, None] * inv_freq, col[:, None] * inv_freq], axis=-1)
    return jnp.cos(ang), jnp.sin(ang)


def apply_rope(t, cos, sin):
    t1, t2 = jnp.split(t, 2, axis=-1)
    cos = cos.astype(t.dtype)
    sin = sin.astype(t.dtype)
    return jnp.concatenate([t1 * cos - t2 * sin, t2 * cos + t1 * sin], axis=-1)


def ret_project(h, w_in):
    b, n, _ = h.shape
    p = h @ w_in
    q, k, v, g = jnp.split(p, [RET_QK_TOTAL, 2 * RET_QK_TOTAL, 2 * RET_QK_TOTAL + RET_V_TOTAL], axis=-1)
    q = q.reshape(b, n, RET_HEADS, RET_QK_DIM).transpose(0, 2, 1, 3)
    k = k.reshape(b, n, RET_HEADS, RET_QK_DIM).transpose(0, 2, 1, 3) * (RET_QK_DIM ** -0.5)
    v = v.reshape(b, n, RET_HEADS, RET_V_DIM).transpose(0, 2, 1, 3)
    return q, k, v, g


def retention_scan(q, k, v, lg, s0):
    b, h, n, dk = q.shape
    dv = v.shape[-1]
    nc = n // RET_CHUNK
    dt = q.dtype
    pos = jnp.arange(RET_CHUNK, dtype=jnp.float32)
    diff = pos[:, None] - pos[None, :]
    intra = jnp.where(diff >= 0, jnp.exp(lg[:, None, None] * jnp.maximum(diff, 0.0)), 0.0).astype(dt)
    q_dec = jnp.exp(lg[:, None] * (pos + 1.0))[:, :, None].astype(dt)
    k_dec = jnp.exp(lg[:, None] * (RET_CHUNK - 1.0 - pos))[:, :, None].astype(dt)
    chunk_dec = jnp.exp(lg * RET_CHUNK)[:, None, None].astype(dt)

    def to_chunks(t):
        return jnp.moveaxis(t.reshape(b, h, nc, RET_CHUNK, t.shape[-1]), 2, 0)

    def step(s, qkv):
        qc, kc, vc = qkv
        scores = jnp.einsum('bhid,bhjd->bhij', qc, kc) * intra
        y = (jnp.einsum('bhij,bhje->bhie', scores, vc)
             + jnp.einsum('bhid,bhde->bhie', qc * q_dec, s))
        s = chunk_dec * s + jnp.einsum('bhjd,bhje->bhde', kc * k_dec, vc)
        return s, y

    s_final, y = lax.scan(step, s0, (to_chunks(q), to_chunks(k), to_chunks(v)))
    y = jnp.moveaxis(y, 0, 2).reshape(b, h, n, dv)
    return y, s_final


def bidirectional_retention(q, k, v, lg_fwd, lg_bwd, s_fwd, s_bwd):
    y_f, s_f = retention_scan(q, k, v, lg_fwd, s_fwd)
    y_b_rev, s_b = retention_scan(jnp.flip(q, 2), jnp.flip(k, 2), jnp.flip(v, 2), lg_bwd, s_bwd)
    return y_f + jnp.flip(y_b_rev, 2), s_f, s_b


def ret_output(y, g, w_out):
    b, h, n, dv = y.shape
    yf = y.astype(jnp.float32)
    mu = jnp.mean(yf, axis=-1, keepdims=True)
    var = jnp.mean(jnp.square(yf - mu), axis=-1, keepdims=True)
    yn = ((yf - mu) * lax.rsqrt(var + NORM_EPS)).astype(g.dtype)
    yn = yn.transpose(0, 2, 1, 3).reshape(b, n, h * dv)
    return (jax.nn.silu(g) * yn) @ w_out


def retention_layer(h_lat, h_ctx, w_in, w_out, decay, cos, sin, need_ctx_out):
    lg = -jnp.abs(decay.astype(jnp.float32))
    qc, kc, vc, gc = ret_project(h_ctx, w_in)
    zeros = jnp.zeros((h_ctx.shape[0], RET_HEADS, RET_QK_DIM, RET_V_DIM), vc.dtype)
    yc, s_f, s_b = bidirectional_retention(qc, kc, vc, lg[0], lg[1], zeros, zeros)
    ql, kl, vl, gl = ret_project(h_lat, w_in)
    ql = apply_rope(ql, cos, sin)
    kl = apply_rope(kl, cos, sin)
    yl, _, _ = bidirectional_retention(ql, kl, vl, lg[0], lg[1], s_f, s_b)
    y_lat = ret_output(yl, gl, w_out)
    y_ctx = ret_output(yc, gc, w_out) if need_ctx_out else None
    return y_lat, y_ctx


def setup_inputs(seed: int = 0) -> dict:
    key = jax.random.key(seed)
    ks = jax.random.split(key, 16)
    f32 = jnp.float32
    base_rate = -np.log(1.0 - 2.0 ** (-5.0 - np.arange(RET_HEADS, dtype=np.float32)))
    base_rate = jnp.asarray(base_rate, dtype=f32)
    return {
        "x": jax.random.normal(ks[0], (BATCH, SEQ, D_MODEL), f32),
        "c": jax.random.normal(ks[1], (BATCH, D_MODEL), f32),
        "ctx": jax.random.normal(ks[2], (BATCH, CTX_LEN, D_MODEL), f32),
        "c_ctx": jax.random.normal(ks[3], (D_MODEL,), f32),
        "ada_w": jax.random.normal(ks[4], (DEPTH, D_MODEL, N_MOD * D_MODEL), f32) * (0.5 * D_MODEL ** -0.5),
        "ada_b": jax.random.normal(ks[5], (DEPTH, N_MOD * D_MODEL), f32) * 0.01,
        "norm_g": 1.0 + 0.02 * jax.random.normal(ks[6], (DEPTH, 3, D_MODEL), f32),
        "final_g": 1.0 + 0.02 * jax.random.normal(ks[7], (D_MODEL,), f32),
        "ffn_w_gate": jax.random.normal(ks[8], (DEPTH, 2, D_MODEL, FFN_DIM), f32) * D_MODEL ** -0.5,
        "ffn_w_up": jax.random.normal(ks[9], (DEPTH, 2, D_MODEL, FFN_DIM), f32) * D_MODEL ** -0.5,
        "ffn_w_down": jax.random.normal(ks[10], (DEPTH, 2, FFN_DIM, D_MODEL), f32) * FFN_DIM ** -0.5,
        "four_w": jax.random.normal(ks[11], (N_FOURIER_LAYERS, D_MODEL, D_MODEL), f32) * D_MODEL ** -0.5,
        "four_b": jax.random.normal(ks[12], (N_FOURIER_LAYERS, D_MODEL), f32) * 0.01,
        "ret_w_in": jax.random.normal(ks[13], (N_RET_LAYERS, D_MODEL, RET_IN_DIM), f32) * D_MODEL ** -0.5,
        "ret_w_out": jax.random.normal(ks[14], (N_RET_LAYERS, RET_V_TOTAL, D_MODEL), f32) * RET_V_TOTAL ** -0.5,
        "ret_decay": base_rate * (1.0 + 0.05 * jax.random.normal(ks[15], (N_RET_LAYERS, 2, RET_HEADS), f32)),
    }


def reference(x, c, ctx, c_ctx, ada_w, ada_b, norm_g, final_g, ffn_w_gate, ffn_w_up, ffn_w_down,
              four_w, four_b, ret_w_in, ret_w_out, ret_decay):
    cos, sin = axial_rope(x.shape[1])
    silu_c = jax.nn.silu(c)
    silu_cc = jax.nn.silu(c_ctx)
    for l in range(DEPTH):
        last = l == DEPTH - 1
        mixer = l % N_MIXERS
        slot = l // N_MIXERS
        ctx_read = (not last) or (mixer == 1)
        m_lat = jnp.split((silu_c @ ada_w[l] + ada_b[l])[:, None, :], N_MOD, axis=-1)
        m_ctx = jnp.split(silu_cc @ ada_w[l] + ada_b[l], N_MOD, axis=-1)

        def half_ffn(t, m, j):
            h = modulate(rmsnorm(t, norm_g[l, j]), m[3 * j], m[3 * j + 1])
            return t + 0.5 * m[3 * j + 2] * swiglu(h, ffn_w_gate[l, j // 2], ffn_w_up[l, j // 2],
                                                  ffn_w_down[l, j // 2])

        x = half_ffn(x, m_lat, 0)
        if ctx_read:
            ctx = half_ffn(ctx, m_ctx, 0)

        h_lat = modulate(rmsnorm(x, norm_g[l, 1]), m_lat[3], m_lat[4])
        h_ctx = modulate(rmsnorm(ctx, norm_g[l, 1]), m_ctx[3], m_ctx[4]) if ctx_read else None
        if mixer == 0:
            y_lat, y_ctx = fourier_layer(h_lat, h_ctx, four_w[slot], four_b[slot])
        else:
            y_lat, y_ctx = retention_layer(h_lat, h_ctx, ret_w_in[slot], ret_w_out[slot], ret_decay[slot],
                                           cos, sin, not last)
        x = x + m_lat[5] * y_lat
        if not last:
            ctx = ctx + m_ctx[5] * y_ctx

        x = half_ffn(x, m_lat, 2)
        if not last:
            ctx = half_ffn(ctx, m_ctx, 2)
    return rmsnorm(x, final_g)
```

```python
import numpy as np
import ml_dtypes
from concourse.bass_utils import run_bass_kernel_spmd
import concourse.bass as bass
import concourse.mybir as mybir

F32 = mybir.dt.float32
BF16 = mybir.dt.bfloat16
AF = mybir.ActivationFunctionType
ALU = mybir.AluOpType
AX = mybir.AxisListType


class Region:
    __slots__ = ("tile", "sub")

    def __init__(self, tile, sub):
        self.tile = tile
        self.sub = sub


class Tile:
    def __init__(self, prog, name, t, space):
        self.prog = prog
        self.name = name
        self.t = t
        self.space = space
        self.whole_writer = None
        self.sub_writers = {}
        self.whole_readers = []
        self.sub_readers = {}
        self.dma_sems = {}

    def __getitem__(self, k):
        return self.t[k]

    def sub(self, k):
        return Region(self, k)


class Prog:
    SEM_LIMIT = 30000

    def __init__(self, nc):
        self.nc = nc
        self.eng = {"pe": nc.tensor, "act": nc.scalar, "dve": nc.vector, "pool": nc.gpsimd, "sp": nc.sync}
        self.sem = {}
        self.cnt = {}
        self.known = {k: {} for k in self.eng}
        self.semobjs = {}
        self.nsem = 0
        self.ctxs = []
        self.out_events = []
        for k in self.eng:
            self._new_engine_sem(k)

    def _alloc_sem(self, name):
        cm = self.nc.semaphore(name)
        s = cm.__enter__()
        self.ctxs.append(cm)
        self.nsem += 1
        sid = self.nsem
        self.semobjs[sid] = s
        return sid

    def _new_engine_sem(self, k):
        self.sem[k] = self._alloc_sem(f"s_{k}_{self.nsem}")
        self.cnt[k] = 0

    def sbuf(self, name, shape, dt):
        cm = self.nc.sbuf_tensor(name, list(shape), dt)
        t = cm.__enter__()
        self.ctxs.append(cm)
        return Tile(self, name, t, "sbuf")

    def psum(self, name, shape, dt):
        cm = self.nc.psum_tensor(name, list(shape), dt)
        t = cm.__enter__()
        self.ctxs.append(cm)
        return Tile(self, name, t, "psum")

    def dram(self, name, shape, dt, kind):
        t = self.nc.dram_tensor(name, list(shape), dt, kind=kind).ap()
        return Tile(self, name, t, "dram")

    def close(self):
        for cm in reversed(self.ctxs):
            cm.__exit__(None, None, None)
        self.ctxs = []

    @staticmethod
    def _reg(r):
        if isinstance(r, Tile):
            return r, None
        return r.tile, r.sub

    def _deps(self, reads, writes):
        deps = []
        for r in reads:
            t, s = self._reg(r)
            if t.whole_writer is not None:
                deps.append(t.whole_writer)
            if s is None:
                deps.extend(t.sub_writers.values())
            elif s in t.sub_writers:
                deps.append(t.sub_writers[s])
        for w in writes:
            t, s = self._reg(w)
            if t.whole_writer is not None:
                deps.append(t.whole_writer)
            deps.extend(t.whole_readers)
            if s is None:
                deps.extend(t.sub_writers.values())
                for l in t.sub_readers.values():
                    deps.extend(l)
            else:
                if s in t.sub_writers:
                    deps.append(t.sub_writers[s])
                deps.extend(t.sub_readers.get(s, ()))
        return deps

    def _record(self, ev, reads, writes):
        for r in reads:
            t, s = self._reg(r)
            if s is None:
                t.whole_readers.append(ev)
            else:
                t.sub_readers.setdefault(s, []).append(ev)
        for w in writes:
            t, s = self._reg(w)
            if s is None:
                t.whole_writer = ev
                t.sub_writers = {}
                t.whole_readers = []
                t.sub_readers = {}
            else:
                t.sub_writers[s] = ev
                t.sub_readers[s] = []

    def _emit_waits(self, k, deps, skip_pe_acc=False):
        e = self.eng[k]
        need = {}
        for ev in deps:
            kind, sid, val, src = ev
            if kind == "dma":
                val = src[1] * 16
            elif skip_pe_acc and src == "pe":
                continue
            if need.get(sid, 0) < val:
                need[sid] = val
        kn = self.known[k]
        for sid, val in need.items():
            if kn.get(sid, 0) >= val:
                continue
            e.wait_ge(self.semobjs[sid], val)
            kn[sid] = val

    def op(self, k, fn, reads=(), writes=(), pe_acc=False):
        deps = self._deps(reads, writes)
        self._emit_waits(k, deps, skip_pe_acc=pe_acc)
        if self.cnt[k] >= self.SEM_LIMIT:
            self._new_engine_sem(k)
        ins = fn(self.eng[k])
        self.cnt[k] += 1
        sid = self.sem[k]
        ins.then_inc(self.semobjs[sid], 1)
        ev = ("c", sid, self.cnt[k], k)
        self._record(ev, reads, writes)
        return ev

    def dma(self, q, out_ap, in_ap, reads, writes, sem_tile, is_output=False, sem_key=None, **kw):
        deps = self._deps(reads, writes)
        self._emit_waits(q, deps)
        rec = sem_tile.dma_sems.get(sem_key)
        if rec is None:
            rec = [self._alloc_sem(f"d_{sem_tile.name}_{len(sem_tile.dma_sems)}"), 0]
            sem_tile.dma_sems[sem_key] = rec
        ins = self.eng[q].dma_start(out=out_ap, in_=in_ap, **kw)
        ins.then_inc(self.semobjs[rec[0]], 16)
        rec[1] += 1
        ev = ("dma", rec[0], rec[1] * 16, rec)
        self._record(ev, reads, writes)
        if is_output:
            self.out_events.append(ev)
        return ev

    def finish(self, q="sp"):
        self._emit_waits(q, self.out_events)


D = 1024
F = 2816
KC = 8
FC = 22
EPS = 1e-6


def cast_op(P, i, out_ap, in_ap, reads, writes, psum=False):
    k = ("act", "dve")[i % 2] if psum else ("act", "dve", "pool")[i % 3]

    if k == "act":
        return P.op("act", lambda e: e.copy(out_ap, in_ap), reads, writes)
    return P.op(k, lambda e: e.tensor_copy(out_ap, in_ap), reads, writes)


def load_cast_weight(P, w, dst, K, N, stages, ctr, scale=None):
    pieces = [(o, min(512, N - o)) for o in range(0, N, 512)]
    for kc in range(K // 128):
        for (o, n) in pieces:
            st = stages[ctr[0] % len(stages)]
            P.dma("sp", st[:, :n], w[kc * 128:(kc + 1) * 128, o:o + n], reads=[], writes=[st], sem_tile=st)
            if scale is None:
                cast_op(P, ctr[0], dst[:, kc, o:o + n], st[:, :n], [st], [dst.sub(("w", kc, o))])
            else:
                P.op("dve", lambda e: e.tensor_scalar(dst[:, kc, o:o + n], st[:, :n], scale, None, ALU.mult),
                     [st], [dst.sub(("w", kc, o))])
            ctr[0] += 1


class _Stop(Exception):
    pass


def build_tok(NT_lat, NT_ctx, ffn=True, mix_k=0, norm_out=None, T=256, stop=0):
    nc = bass.Bass("TRN2", target_bir_lowering=False)
    P = Prog(nc)
    NT = NT_lat + NT_ctx
    x = P.dram("x", [NT, D], F32, "ExternalInput")
    ident_d = P.dram("ident", [128, 128], BF16, "ExternalInput")
    pp_d = P.dram("pp", [128, 10, KC], F32, "ExternalInput")
    rows_d = P.dram("rows", [6, D], F32, "ExternalInput")
    if norm_out != "final":
        y = P.dram("y", [NT, D], F32, "ExternalOutput")
    if ffn:
        wg_d = P.dram("wg", [D, F], F32, "ExternalInput")
        wu_d = P.dram("wu", [D, F], F32, "ExternalInput")
        wd_d = P.dram("wd", [F, D], F32, "ExternalInput")
    if mix_k:
        u_d = P.dram("u", [NT, mix_k], BF16 if mix_k == 2048 else F32, "ExternalInput")
        wm_d = P.dram("wm", [mix_k, D], F32, "ExternalInput")
    if norm_out == "hT":
        hT_d = P.dram("hT", [KC, 128, NT], BF16, "ExternalOutput")
    if norm_out == "pq":
        cs_d = P.dram("cs", [128, 2, 512], BF16, "ExternalInput")
        pq_d = P.dram("pq", [NT, 4, 512], BF16, "ExternalOutput")
    if norm_out == "final":
        yf = P.dram("yf", [NT, D], F32, "ExternalOutput")

    ident = P.sbuf("identb", [128, 128], BF16)
    P.dma("sp", ident[:], ident_d[:], [], [ident], ident)
    pp = P.sbuf("ppv", [128, 10, KC], F32)
    P.dma("sp", pp[:], pp_d[:], [], [pp], pp)
    gsv = P.sbuf("gsv", [128, 4, KC], F32)
    for (dst, gi, si) in ((0, 0, 2), (1, 0, 4), (2, 5, 7), (3, 5, 9)):
        P.op("dve", lambda e: e.tensor_tensor(gsv[:, dst, :], pp[:, gi, :], pp[:, si, :], ALU.mult), [pp], [gsv.sub(dst)])
        P.op("dve", lambda e: e.tensor_tensor(gsv[:, dst, :], gsv[:, dst, :], pp[:, gi, :], ALU.add), [pp, gsv.sub(dst)], [gsv.sub(dst)])
    need_rows = ([0, 1] if ffn else []) + ([2, 3, 4] if mix_k else []) + ([5] if norm_out == "final" else [])
    RS = {r: i for i, r in enumerate(need_rows)}
    rows = P.sbuf("rowsb", [128, max(1, len(need_rows)), D], F32)
    for r in need_rows:
        P.dma("sp", rows[:, RS[r], :], rows_d[r:r + 1, :].partition_broadcast(128), [], [rows.sub(RS[r])], rows)
    if ffn:
        for r in (0, 1):
            P.op("pool", lambda e: e.tensor_scalar(rows[:, RS[r], :], rows[:, RS[r], :], 0.5, None, ALU.mult), [rows.sub(RS[r])], [rows.sub(RS[r])])

    stages = [P.sbuf(f"stg{i}", [128, 512], F32) for i in range(4)]
    ctr = [0]
    if ffn:
        wg = P.sbuf("wg_b", [128, KC, F], BF16)
        wu = P.sbuf("wu_b", [128, KC, F], BF16)
        wd = P.sbuf("wd_b", [128, FC, D], BF16)
        load_cast_weight(P, wg_d, wg, D, F, stages, ctr)
        load_cast_weight(P, wu_d, wu, D, F, stages, ctr)
        load_cast_weight(P, wd_d, wd, F, D, stages, ctr)
    if mix_k:
        MK = mix_k // 128
        wm = P.sbuf("wm_b", [128, MK, D], BF16)
        load_cast_weight(P, wm_d, wm, mix_k, D, stages, ctr)
    if norm_out == "pq":
        cs = P.sbuf("csb", [128, 2, 512], BF16)
        P.dma("sp", cs[:], cs_d[:], [], [cs], cs)

    if stop == 1:
        P.dma("sp", y[0:128, :], rows[:, 0, :], [rows], [], rows, is_output=True)
        P.finish(); P.close(); return nc
    NTL = T // 128
    xt = [P.sbuf(f"xt{i}", [128, NTL, D], F32) for i in range(2)]
    xn = P.sbuf("xn", [128, NTL, D], BF16)
    st = P.sbuf("st", [128, 4 * NTL], F32)
    psT = P.psum("psT", [128, KC, T], BF16)
    hT = P.sbuf("hT_s", [128, KC, T], BF16)
    if ffn:
        pgu = [P.psum(f"pgu{i}", [128, 512], F32) for i in range(4)]
        sig = [P.sbuf(f"sig{i}", [128, T], F32) for i in range(2)]
        actT = P.sbuf("actT", [128, FC, T], BF16)
    po = [P.psum(f"po{i}", [128, 512], F32) for i in range(2)]
    dtmp = [P.sbuf(f"dtmp{i}", [128, 512], F32) for i in range(2)]
    if mix_k:
        ut = P.sbuf("ut", [128, NTL, mix_k], BF16 if mix_k == 2048 else F32)
        ub = P.sbuf("ub", [128, NTL, mix_k], BF16) if mix_k != 2048 else ut
        uT = P.sbuf("uT", [128, MK, T], BF16)
    if norm_out == "pq":
        pqs = P.sbuf("pqs", [128, 4, 512], BF16)
    if norm_out == "final":
        yo = P.sbuf("yo", [128, NTL, D], F32)
    rr = [0]

    def norm_T(X, tiles, TT, gsi, shi, dstT):
        for ti, (o, n) in enumerate(tiles):
            P.op("act", lambda e: e.activation(out=xn[:n, ti, :], in_=X[:n, ti, :], func=AF.Square,
                                                accum_out=st[:n, 4 * ti:4 * ti + 1]),
                 [X.sub(ti)], [xn.sub(ti), st.sub(ti)])
            P.op("dve", lambda e: e.tensor_scalar(st[:n, 4 * ti + 1:4 * ti + 2], st[:n, 4 * ti:4 * ti + 1],
                                                   1.0 / D, EPS, ALU.mult, ALU.add), [st.sub(ti)], [st.sub(ti)])
            P.op("act", lambda e: e.activation(out=st[:n, 4 * ti + 2:4 * ti + 3], in_=st[:n, 4 * ti + 1:4 * ti + 2],
                                                func=AF.Sqrt), [st.sub(ti)], [st.sub(ti)])
            P.op("dve", lambda e: e.reciprocal(st[:n, 4 * ti + 3:4 * ti + 4], st[:n, 4 * ti + 2:4 * ti + 3]),
                 [st.sub(ti)], [st.sub(ti)])
            P.op("dve", lambda e: e.tensor_scalar(xn[:n, ti, :], X[:n, ti, :], st[:n, 4 * ti + 3:4 * ti + 4], None,
                                                   ALU.mult), [X.sub(ti), st.sub(ti)], [xn.sub(ti)])
            for kc in range(KC):
                P.op("pe", lambda e: e.transpose(psT[:, kc, o:o + n], xn[:n, ti, kc * 128:(kc + 1) * 128],
                                                 ident[:n, :n]),
                     [xn.sub(ti), ident], [psT.sub(kc // 4)], pe_acc=True)
        for kc in range(KC):
            rds = [psT.sub(kc // 4)]
            if gsi is None:
                cast_op(P, kc // 4, dstT[:, kc, :TT], psT[:, kc, :TT], rds, [dstT.sub(kc)], psum=True)
            elif kc // 4 == 0:
                P.op("act", lambda e: e.activation(out=dstT[:, kc, :TT], in_=psT[:, kc, :TT], func=AF.Identity,
                                                    scale=gsv[:, gsi, kc:kc + 1], bias=pp[:, shi, kc:kc + 1]),
                     rds + [gsv, pp], [dstT.sub(kc)])
            else:
                P.op("dve", lambda e: e.tensor_scalar(dstT[:, kc, :TT], psT[:, kc, :TT], gsv[:, gsi, kc:kc + 1],
                                                       pp[:, shi, kc:kc + 1], ALU.mult, ALU.add),
                     rds + [gsv, pp], [dstT.sub(kc)])

    groups = [(g0, min(T, NT_lat - g0), 0) for g0 in range(0, NT_lat, T)]
    groups += [(NT_lat + g0, min(T, NT_ctx - g0), 1) for g0 in range(0, NT_ctx, T)]
    for gi, (tok0, TT, isctx) in enumerate(groups):
        X = xt[gi % 2]
        tiles = [(o, min(128, TT - o)) for o in range(0, TT, 128)]
        for ti, (o, n) in enumerate(tiles):
            P.dma("sp", X[:n, ti, :], x[tok0 + o:tok0 + o + n, :], [], [X.sub(ti)], X, sem_key=ti)
        if mix_k:
            for ti, (o, n) in enumerate(tiles):
                P.dma("sp", ut[:n, ti, :], u_d[tok0 + o:tok0 + o + n, :], [], [ut.sub(ti)], ut, sem_key=ti)
                if ub is not ut:
                    cast_op(P, ti, ub[:n, ti, :], ut[:n, ti, :], [ut.sub(ti)], [ub.sub(ti)])
                if stop == 3:
                    P.dma("sp", y[0:128, :], X[:, 0, :], [X, ub], [], X, is_output=True, sem_key=0)
                    P.finish(); P.close(); return nc
                for k0 in range(0, MK, KC):
                    for kk in range(KC):
                        kc = k0 + kk
                        P.op("pe", lambda e: e.transpose(psT[:, kk, o:o + n], ub[:n, ti, kc * 128:(kc + 1) * 128],
                                                         ident[:n, :n]),
                             [ub.sub(ti), ident], [psT.sub(kk // 4)], pe_acc=True)
                    if stop == 4:
                        P.dma("sp", y[0:128, :], X[:, 0, :], [X, psT], [], X, is_output=True, sem_key=0)
                        P.finish(); P.close(); return nc
                    for kk in range(KC):
                        kc = k0 + kk
                        cast_op(P, kk // 4, uT[:, kc, o:o + n], psT[:, kk, o:o + n], [psT.sub(kk // 4)], [uT.sub((kc, ti))], psum=True)
            if stop == 2:
                P.dma("sp", y[0:128, :], X[:, 0, :], [X, uT], [], X, is_output=True, sem_key=0)
                P.finish(); P.close(); return nc
            for ti, (o, n) in enumerate(tiles):
                for half in range(2):
                    pb = po[rr[0] % 2]
                    dt_ = dtmp[rr[0] % 2]
                    rr[0] += 1
                    for kc in range(MK):
                        P.op("pe", lambda e: e.matmul(pb[:n, :], uT[:, kc, o:o + n], wm[:, kc, half * 512:(half + 1) * 512],
                                                      start=(kc == 0), stop=(kc == MK - 1)),
                             [uT.sub((kc, ti)), wm], [pb], pe_acc=True)
                    hs = slice(half * 512, (half + 1) * 512)
                    P.op("dve", lambda e: e.tensor_tensor(dt_[:n, :], pb[:n, :], rows[:n, RS[4], hs], ALU.add),
                         [pb, rows.sub(RS[4])], [dt_])
                    P.op("pool", lambda e: e.tensor_tensor(dt_[:n, :], dt_[:n, :], rows[:n, RS[2 + isctx], hs], ALU.mult),
                         [dt_, rows.sub(RS[2 + isctx])], [dt_])
                    P.op("pool", lambda e: e.tensor_tensor(X[:n, ti, hs], X[:n, ti, hs], dt_[:n, :], ALU.add),
                         [dt_, X.sub(ti)], [X.sub(ti)])
        if ffn:
            norm_T(X, tiles, TT, 0 + isctx, 1 + 2 * isctx, hT)
            for fc in range(FC):
                pg = pgu[2 * (fc % 2)]
                pu = pgu[2 * (fc % 2) + 1]
                sg = sig[fc % 2]
                for kc in range(KC):
                    P.op("pe", lambda e: e.matmul(pg[:, :TT], wg[:, kc, fc * 128:(fc + 1) * 128], hT[:, kc, :TT],
                                                  start=(kc == 0), stop=(kc == KC - 1)),
                         [wg, hT.sub(kc)], [pg], pe_acc=True)
                for kc in range(KC):
                    P.op("pe", lambda e: e.matmul(pu[:, :TT], wu[:, kc, fc * 128:(fc + 1) * 128], hT[:, kc, :TT],
                                                  start=(kc == 0), stop=(kc == KC - 1)),
                         [wu, hT.sub(kc)], [pu], pe_acc=True)
                P.op("act", lambda e: e.activation(out=sg[:, :TT], in_=pg[:, :TT], func=AF.Silu), [pg], [sg])
                P.op("dve", lambda e: e.tensor_tensor(actT[:, fc, :TT], pu[:, :TT], sg[:, :TT], ALU.mult),
                     [pu, sg], [actT.sub(fc)])
            for ti, (o, n) in enumerate(tiles):
                for half in range(2):
                    pb = po[rr[0] % 2]
                    dt_ = dtmp[rr[0] % 2]
                    rr[0] += 1
                    hs = slice(half * 512, (half + 1) * 512)
                    for fc in range(FC):
                        P.op("pe", lambda e: e.matmul(pb[:n, :], actT[:, fc, o:o + n], wd[:, fc, hs],
                                                      start=(fc == 0), stop=(fc == FC - 1)),
                             [actT.sub(fc), wd], [pb], pe_acc=True)
                    P.op("dve", lambda e: e.tensor_tensor(dt_[:n, :], pb[:n, :], rows[:n, RS[isctx], hs], ALU.mult),
                         [pb, rows.sub(RS[isctx])], [dt_])
                    P.op("pool", lambda e: e.tensor_tensor(X[:n, ti, hs], X[:n, ti, hs], dt_[:n, :], ALU.add),
                         [dt_, X.sub(ti)], [X.sub(ti)])
        if norm_out != "final":
            for ti, (o, n) in enumerate(tiles):
                P.dma("sp", y[tok0 + o:tok0 + o + n, :], X[:n, ti, :], [X.sub(ti)], [], X, is_output=True, sem_key=ti)
        if norm_out in ("hT", "pq"):
            norm_T(X, tiles, TT, 2 + isctx, 6 + 2 * isctx, hT)
        if norm_out == "hT":
            for kc in range(KC):
                P.dma("sp", hT_d[kc, :, tok0:tok0 + TT], hT[:, kc, :TT], [hT.sub(kc)], [], hT, is_output=True, sem_key=kc)
        if norm_out == "pq":
            for ti, (o, n) in enumerate(tiles):
                for g in range(4):
                    pb = po[rr[0] % 2]
                    rr[0] += 1
                    for c2 in range(2):
                        kc = g * 2 + c2
                        P.op("pe", lambda e: e.matmul(pb[:n, :], hT[:, kc, o:o + n], cs[:, c2, :],
                                                      start=(c2 == 0), stop=(c2 == 1)),
                             [hT.sub(kc), cs], [pb], pe_acc=True)
                    cast_op(P, g, pqs[:n, g, :], pb[:n, :], [pb], [pqs.sub(g)], psum=True)
                P.dma("sp", pq_d[tok0 + o:tok0 + o + n, :, :], pqs[:n, :, :], [pqs], [], pqs, is_output=True)
        if norm_out == "final":
            for ti, (o, n) in enumerate(tiles):
                P.op("act", lambda e: e.activation(out=yo[:n, ti, :], in_=X[:n, ti, :], func=AF.Square,
                                                    accum_out=st[:n, 4 * ti:4 * ti + 1]),
                     [X.sub(ti)], [yo.sub(ti), st.sub(ti)])
                P.op("dve", lambda e: e.tensor_scalar(st[:n, 4 * ti + 1:4 * ti + 2], st[:n, 4 * ti:4 * ti + 1],
                                                       1.0 / D, EPS, ALU.mult, ALU.add), [st.sub(ti)], [st.sub(ti)])
                P.op("act", lambda e: e.activation(out=st[:n, 4 * ti + 2:4 * ti + 3], in_=st[:n, 4 * ti + 1:4 * ti + 2],
                                                    func=AF.Sqrt), [st.sub(ti)], [st.sub(ti)])
                P.op("dve", lambda e: e.reciprocal(st[:n, 4 * ti + 3:4 * ti + 4], st[:n, 4 * ti + 2:4 * ti + 3]),
                     [st.sub(ti)], [st.sub(ti)])
                P.op("dve", lambda e: e.scalar_tensor_tensor(yo[:n, ti, :], X[:n, ti, :], st[:n, 4 * ti + 3:4 * ti + 4],
                                                              rows[:n, RS[5], :], ALU.mult, ALU.mult),
                     [X.sub(ti), st.sub(ti), rows.sub(RS[5])], [yo.sub(ti)])
                P.dma("sp", yf[tok0 + o:tok0 + o + n, :], yo[:n, ti, :], [yo.sub(ti)], [], yo, is_output=True, sem_key=ti)
    P.finish()
    P.close()
    return nc


MCH = 64
NMC = 256 // MCH


def build_four():
    nc = bass.Bass("TRN2", target_bir_lowering=False)
    P = Prog(nc)
    z_d = P.dram("z", [NMC, 128, 2 * 128 * MCH], BF16, "ExternalInput")
    zc_d = P.dram("zc", [128, 2 * 2 * 256], BF16, "ExternalInput")
    dft_d = P.dram("dft", [128, 2 * 256], BF16, "ExternalInput")
    tw_d = P.dram("tw", [128, 2 * 128 * 128], BF16, "ExternalInput")
    c256_d = P.dram("c256", [128, 2 * 2 * 256], BF16, "ExternalInput")
    r_d = P.dram("r", [NMC, 128, 128 * MCH], F32, "ExternalOutput")
    rc_d = P.dram("rc", [128, 2 * 256], F32, "ExternalOutput")

    dft = P.sbuf("dft_s", [128, 2, 256], BF16)
    P.dma("sp", dft[:], dft_d[:].rearrange("p (s k) -> p s k", s=2), [], [dft], dft)
    tw = P.sbuf("tw_s", [128, 2, 128, 128], BF16)
    for s in range(2):
        for q in range(4):
            P.dma("sp", tw[:, s, q * 32:(q + 1) * 32, :],
                  tw_d[:, (s * 128 + q * 32) * 128:(s * 128 + q * 32 + 32) * 128].rearrange("p (k j) -> p k j", j=128),
                  [], [tw], tw)
    c256 = P.sbuf("c256_s", [128, 2, 2, 256], BF16)
    P.dma("sp", c256[:], c256_d[:].rearrange("p (a b k) -> p a b k", a=2, b=2), [], [c256], c256)
    zc = P.sbuf("zc_s", [128, 2, 2, 256], BF16)
    P.dma("sp", zc[:], zc_d[:].rearrange("p (a b k) -> p a b k", a=2, b=2), [], [zc], zc)

    Z = [P.sbuf(f"Z{i}", [128, 2, 128, MCH], BF16) for i in range(2)]
    A = P.sbuf("A_s", [128, 2, 128, MCH], BF16)
    R = P.sbuf("R_s", [128, 128, MCH], F32)
    rcs = P.sbuf("rc_s", [128, 2, 256], F32)
    ps = [P.psum(f"ps{i}", [128, 512], F32) for i in range(6)]
    pi = [0]

    def nxt():
        b = ps[pi[0] % len(ps)]
        pi[0] += 1
        return b

    for kh in range(2):
        pb = nxt()
        i = 0
        for nch in range(2):
            for sel in range(2):
                P.op("pe", lambda e: e.matmul(pb[:, :256], c256[:, nch, sel, kh * 128:(kh + 1) * 128], zc[:, nch, sel, :],
                                              start=(i == 0), stop=(i == 3)), [c256, zc], [pb], pe_acc=True)
                i += 1
        P.op("act", lambda e: e.activation(out=rcs[:, kh, :], in_=pb[:, :256], func=AF.Copy, scale=1.0 / 256.0),
             [pb], [rcs.sub(kh)])
    P.dma("sp", rc_d[:].rearrange("p (a k) -> p a k", a=2), rcs[:], [rcs], [], rcs, is_output=True)

    ev = 0
    for mc in range(NMC):
        Zt = Z[mc % 2]
        for part in range(2):
            P.dma("sp", Zt[:, part, :, :],
                  z_d[mc, :, part * 128 * MCH:(part + 1) * 128 * MCH].rearrange("p (n m) -> p n m", m=MCH),
                  [], [Zt.sub(part)], Zt, sem_key=part)
        for mi in range(MCH):
            if mi % 2 == 0:
                pb = nxt()
            off = (mi % 2) * 256
            for part in range(2):
                P.op("pe", lambda e: e.matmul(pb[:, off:off + 256], Zt[:, part, :, mi], dft[:, part, :],
                                              start=(part == 0), stop=(part == 1)),
                     [Zt.sub(part), dft], [pb], pe_acc=True)
            src = pb[:, off:off + 256].rearrange("p (r k) -> p r k", r=2)
            if (ev // 2) % 2 == 0:
                P.op("act", lambda e: e.copy(A[:, :, :, mi], src), [pb], [A.sub(mi)])
            else:
                P.op("dve", lambda e: e.tensor_copy(A[:, :, :, mi], src), [pb], [A.sub(mi)])
            ev += 1
        per = 512 // MCH
        for k1 in range(128):
            if k1 % per == 0:
                pb = nxt()
            off = (k1 % per) * MCH
            for sel in range(2):
                P.op("pe", lambda e: e.matmul(pb[:, off:off + MCH], tw[:, sel, k1, :], A[:, sel, k1, :],
                                              start=(sel == 0), stop=(sel == 1)),
                     [tw, A], [pb], pe_acc=True)
            if k1 % per == per - 1:
                k0 = k1 - per + 1
                dst = R[:, k0:k0 + per, :].rearrange("p k m -> p (k m)")
                if ev % 2 == 0:
                    P.op("act", lambda e: e.activation(out=dst, in_=pb[:, :], func=AF.Copy, scale=1.0 / 2048.0),
                         [pb], [R.sub(k0)])
                else:
                    P.op("dve", lambda e: e.tensor_scalar(dst, pb[:, :], 1.0 / 2048.0, None, ALU.mult), [pb], [R.sub(k0)])
                ev += 1
        P.dma("sp", r_d[mc, :, :], R[:].rearrange("p k m -> p (k m)"), [R], [], R, is_output=True)
    P.finish()
    P.close()
    return nc


def four_consts():
    import ml_dtypes
    bf = ml_dtypes.bfloat16
    n = np.arange(128)
    a = 2 * np.pi * np.outer(n, n) / 128.0
    C, S = np.cos(a), np.sin(a)
    dft = np.stack([np.concatenate([C, S], 1), np.concatenate([-S, C], 1)], 1)
    n2 = np.arange(128)[:, None, None]
    k1 = np.arange(128)[None, :, None]
    k2 = np.arange(128)[None, None, :]
    ang = 2 * np.pi * ((n2 * (k1 + 128 * k2)) % 16384) / 16384.0
    tw = np.stack([np.cos(ang), -np.sin(ang)], 1)
    nn = np.arange(256)
    a2 = 2 * np.pi * ((np.outer(nn, nn)) % 256) / 256.0
    c256 = np.stack([np.cos(a2), -np.sin(a2)], 1)
    c256 = c256.reshape(2, 128, 2, 256).transpose(1, 0, 2, 3)
    return dict(dft=np.ascontiguousarray(dft.reshape(128, 512)).astype(bf),
                tw=np.ascontiguousarray(tw.reshape(128, -1)).astype(bf),
                c256=np.ascontiguousarray(c256.reshape(128, -1)).astype(bf))


def four_layout_in(pq_lat, pq_ctx):
    z = pq_lat.reshape(128, 128, 2, NMC, MCH).transpose(3, 0, 2, 1, 4)
    zc = pq_ctx.reshape(2, 128, 2, 256).transpose(1, 0, 2, 3)
    return np.ascontiguousarray(z.reshape(NMC, 128, -1)), np.ascontiguousarray(zc.reshape(128, -1))


def four_layout_out(r, rc):
    f = r.reshape(NMC, 128, 128, MCH).transpose(1, 2, 0, 3).reshape(16384, 256)
    fc = rc.reshape(128, 2, 256).transpose(1, 0, 2).reshape(256, 256)
    return f, fc


NCOL = 4608


def build_mod():
    nc = bass.Bass("TRN2", target_bir_lowering=False)
    P = Prog(nc)
    w_d = P.dram("w", [1024, NCOL], F32, "ExternalInput")
    b_d = P.dram("b", [1, NCOL], F32, "ExternalInput")
    c_d = P.dram("c", [128, 8 * 3], F32, "ExternalInput")
    m_d = P.dram("m", [3, NCOL], F32, "ExternalOutput")
    cs = P.sbuf("cs", [128, 8, 3], F32)
    P.dma("sp", cs[:], c_d[:].rearrange("p (k r) -> p k r", r=3), [], [cs], cs)
    sc = P.sbuf("sc", [128, 8, 3], F32)
    P.op("act", lambda e: e.activation(out=sc[:], in_=cs[:], func=AF.Silu), [cs], [sc])
    bb = P.sbuf("bb", [3, NCOL], F32)
    P.dma("sp", bb[:], b_d[:, :].partition_broadcast(3), [], [bb], bb)
    W = [P.sbuf(f"W{i}", [128, 8, 512], F32) for i in range(2)]
    ps = [P.psum(f"ps{i}", [128, 512], F32) for i in range(2)]
    mo = P.sbuf("mo", [3, NCOL], F32)
    for j in range(NCOL // 512):
        Wt = W[j % 2]
        pb = ps[j % 2]
        for kc in range(8):
            P.dma("sp", Wt[:, kc, :], w_d[kc * 128:(kc + 1) * 128, j * 512:(j + 1) * 512], [], [Wt.sub(kc)], Wt, sem_key=kc)
        for kc in range(8):
            P.op("pe", lambda e: e.matmul(pb[:3, :], sc[:, kc, :], Wt[:, kc, :], start=(kc == 0), stop=(kc == 7)),
                 [sc, Wt.sub(kc)], [pb], pe_acc=True)
        P.op("dve", lambda e: e.tensor_tensor(mo[:, j * 512:(j + 1) * 512], pb[:3, :], bb[:, j * 512:(j + 1) * 512], ALU.add),
             [pb, bb], [mo.sub(j)])
    P.dma("sp", m_d[:, :], mo[:], [mo], [], mo, is_output=True)
    P.finish()
    P.close()
    return nc


NCTX = 256


def build_ret(NCH_LAT, pass2):
    nc = bass.Bass("TRN2", target_bir_lowering=False)
    P = Prog(nc)
    NLAT = 128 * NCH_LAT
    NTOK = NCTX + NLAT
    hT_d = P.dram("hT", [KC, 128, NTOK], BF16, "ExternalInput")
    win_d = P.dram("win", [1024, 6144], F32, "ExternalInput")
    dec_d = P.dram("dec", [1, 8], F32, "ExternalInput")
    ropec_d = P.dram("ropec", [128, NLAT], F32, "ExternalInput")
    ropes_d = P.dram("ropes", [128, NLAT], F32, "ExternalInput")
    cmask_d = P.dram("cmask", [128, 5 * 128], F32, "ExternalInput")
    qrow_d = P.dram("qrow", [2, 128], F32, "ExternalInput")
    kcol_d = P.dram("kcol", [128, 2], F32, "ExternalInput")
    cexp_d = P.dram("cexp", [1, NCH_LAT + 2], F32, "ExternalInput")
    comb_d = P.dram("comb", [1, 20], F32, "ExternalInput")
    ident_d = P.dram("ident", [128, 128], BF16, "ExternalInput")
    if pass2:
        ain_d = P.dram("ain", [4, 128, 8192], F32, "ExternalInput")
        o_d = P.dram("o", [NTOK, 2048], BF16, "ExternalOutput")
        spill_d = P.dram("spill", [4, NCH_LAT + 2, 128, 1024], BF16, "Internal")
    else:
        aout_d = P.dram("aout", [128, 8192], F32, "ExternalOutput")

    ident = P.sbuf("identb", [128, 128], BF16)
    P.dma("sp", ident[:], ident_d[:], [], [ident], ident)
    cmask = P.sbuf("cmask_s", [128, 5, 128], F32)
    P.dma("sp", cmask[:], cmask_d[:].rearrange("p (a b) -> p a b", a=5), [], [cmask], cmask)
    qrow = P.sbuf("qrow_s", [128, 2, 128], F32)
    for r in range(2):
        P.dma("sp", qrow[:, r, :], qrow_d[r:r + 1, :].partition_broadcast(128), [], [qrow], qrow)
    kcol = P.sbuf("kcol_s", [128, 2], F32)
    P.dma("sp", kcol[:], kcol_d[:], [], [kcol], kcol)
    NCE = NCH_LAT + 2
    cexp = P.sbuf("cexp_s", [128, NCE], F32)
    P.dma("sp", cexp[:], cexp_d[:, :].partition_broadcast(128), [], [cexp], cexp)
    comb = P.sbuf("comb_s", [128, 20], F32)
    P.dma("sp", comb[:], comb_d[:, :].partition_broadcast(128), [], [comb], comb)
    lg = P.sbuf("lg_s", [128, 8], F32)
    P.dma("sp", lg[:], dec_d[:, :].partition_broadcast(128), [], [lg], lg)
    lgn = P.sbuf("lgn_s", [128, 8], F32)
    P.op("dve", lambda e: e.tensor_scalar(lgn[:], lg[:], -1.0, None, ALU.mult), [lg], [lgn])
    P.op("dve", lambda e: e.tensor_tensor(lg[:], lg[:], lgn[:], ALU.min), [lg, lgn], [lg])
    c128 = P.sbuf("c128", [128, 1], F32)
    P.op("dve", lambda e: e.memset(c128[:], 128.0), [], [c128])

    DT = P.sbuf("DT", [128, 4, 128], F32)
    qd = P.sbuf("qd", [128, 4, 2, 128], F32)
    kd = P.sbuf("kd", [128, 4, 2], F32)
    g128 = P.sbuf("g128", [128, 4, 2], F32)
    wA = P.sbuf("wA", [128, 4, 2, NCE], F32)
    wc = P.sbuf("wc", [128, 4, 10], F32)
    tmpa = P.sbuf("tmpa", [128, 128], F32)
    tmpb = P.sbuf("tmpb", [128, 128], F32)
    for h in range(4):
        lf = lg[:, h:h + 1]
        lb = lg[:, 4 + h:5 + h]
        P.op("act", lambda e: e.activation(out=tmpa[:], in_=cmask[:, 0, :], func=AF.Exp, scale=lf), [cmask, lg], [tmpa])
        P.op("dve", lambda e: e.tensor_tensor(DT[:, h, :], tmpa[:], cmask[:, 2, :], ALU.mult), [tmpa, cmask], [DT.sub(h)])
        P.op("act", lambda e: e.activation(out=tmpb[:], in_=cmask[:, 1, :], func=AF.Exp, scale=lb), [cmask, lg], [tmpb])
        P.op("dve", lambda e: e.tensor_tensor(tmpb[:], tmpb[:], cmask[:, 3, :], ALU.mult), [tmpb, cmask], [tmpb])
        P.op("dve", lambda e: e.tensor_tensor(DT[:, h, :], DT[:, h, :], tmpb[:], ALU.add), [tmpb, DT.sub(h)], [DT.sub(h)])
        P.op("dve", lambda e: e.tensor_tensor(DT[:, h, :], DT[:, h, :], cmask[:, 4, :], ALU.add), [cmask, DT.sub(h)], [DT.sub(h)])
        P.op("act", lambda e: e.activation(out=qd[:, h, 0, :], in_=qrow[:, 0, :], func=AF.Exp, scale=lf), [qrow, lg], [qd.sub(h)])
        P.op("act", lambda e: e.activation(out=qd[:, h, 1, :], in_=qrow[:, 1, :], func=AF.Exp, scale=lb), [qrow, lg], [qd.sub(h)])
        P.op("act", lambda e: e.activation(out=kd[:, h, 0:1], in_=kcol[:, 0:1], func=AF.Exp, scale=lf), [kcol, lg], [kd.sub(h)])
        P.op("act", lambda e: e.activation(out=kd[:, h, 1:2], in_=kcol[:, 1:2], func=AF.Exp, scale=lb), [kcol, lg], [kd.sub(h)])
        P.op("act", lambda e: e.activation(out=g128[:, h, 0:1], in_=c128[:], func=AF.Exp, scale=lf), [c128, lg], [g128.sub(h)])
        P.op("act", lambda e: e.activation(out=g128[:, h, 1:2], in_=c128[:], func=AF.Exp, scale=lb), [c128, lg], [g128.sub(h)])
        P.op("act", lambda e: e.activation(out=wA[:, h, 0, :], in_=cexp[:], func=AF.Exp, scale=lf), [cexp, lg], [wA.sub(h)])
        P.op("act", lambda e: e.activation(out=wA[:, h, 1, :], in_=cexp[:], func=AF.Exp, scale=lb), [cexp, lg], [wA.sub(h)])
        P.op("act", lambda e: e.activation(out=wc[:, h, 0:5], in_=comb[:, 0:5], func=AF.Exp, scale=lf), [comb, lg], [wc.sub(h)])
        P.op("act", lambda e: e.activation(out=wc[:, h, 5:10], in_=comb[:, 5:10], func=AF.Exp, scale=lb), [comb, lg], [wc.sub(h)])
        P.op("dve", lambda e: e.tensor_tensor(wc[:, h, :], wc[:, h, :], comb[:, 10:20], ALU.mult), [comb, wc.sub(h)], [wc.sub(h)])

    win = P.sbuf("win_b", [128, KC, 6144], BF16)
    stages = [P.sbuf(f"stg{i}", [128, 512], F32) for i in range(4)]
    ctr = 0
    for kc in range(KC):
        for o in range(0, 6144, 512):
            st = stages[ctr % 4]
            P.dma("sp", st[:], win_d[kc * 128:(kc + 1) * 128, o:o + 512], [], [st], st)
            if 1024 <= o < 2048:
                P.op("dve", lambda e: e.tensor_scalar(win[:, kc, o:o + 512], st[:], 0.0625, None, ALU.mult), [st], [win.sub((kc, o))])
            else:
                k = ("act", "pool", "dve")[ctr % 3]
                if k == "act":
                    P.op("act", lambda e: e.copy(win[:, kc, o:o + 512], st[:]), [st], [win.sub((kc, o))])
                else:
                    P.op(k, lambda e: e.tensor_copy(win[:, kc, o:o + 512], st[:]), [st], [win.sub((kc, o))])
            ctr += 1

    hb = [P.sbuf(f"hb{i}", [128, KC, 512], BF16) for i in range(2)]
    rc = [P.sbuf(f"rc{i}", [128, 512], F32) for i in range(2)]
    rs = [P.sbuf(f"rs{i}", [128, 512], F32) for i in range(2)]
    ra = P.sbuf("ra", [128, 512], F32)
    rb = P.sbuf("rb", [128, 512], F32)
    kTr = P.sbuf("kTr", [128, 2, 512], BF16)
    qTr = P.sbuf("qTr", [128, 2, 512], BF16)
    vb = [P.sbuf(f"vb{i}", [128, 512], BF16) for i in range(2)]
    kf = P.sbuf("kf", [128, 256], BF16)
    kb = P.sbuf("kb", [128, 256], BF16)
    B = [P.psum(f"B{i}", [128, 512], F32) for i in range(7)]
    KTP = P.psum("KTP", [128, 1024], BF16)
    Sb = {s: P.sbuf(f"Sb_{s}", [128, 2, 512], F32) for s in ("ctx", "lat")}
    Af = {s: P.sbuf(f"Af_{s}", [128, 2, 512], F32) for s in ("ctx", "lat")}
    Sb_bf = P.sbuf("Sb_bf", [128, 2, 512], BF16)
    if pass2:
        Sf = P.sbuf("Sf", [128, 2, 512], F32)
        Sf_bf = P.sbuf("Sf_bf", [128, 2, 512], BF16)
        Sbin = P.sbuf("Sbin", [128, 2, 512], F32)
        astg = [P.sbuf(f"astg{i}", [128, 2, 512], F32) for i in range(2)]
        sbl = [P.sbuf(f"sbl{i}", [128, 2, 512], BF16) for i in range(2)]
        Sbt = P.sbuf("Sbt", [128, 2, 512], BF16)
        sg = P.sbuf("sg", [128, 512], F32)
        sT = P.sbuf("sT", [128, 128], BF16)
        qf = P.sbuf("qf", [128, 2, 128], BF16)
        qb = P.sbuf("qb", [128, 2, 128], BF16)
        stt = P.sbuf("stt", [128, 8], F32)
        junk = P.sbuf("junk", [128, 512], BF16)
        yn = P.sbuf("yn", [128, 512], F32)
        ob = [P.sbuf(f"ob{i}", [128, 512], BF16) for i in range(2)]

    bi = [0]
    ci = [0]

    def load_block(tok0, n, lat0, rope):
        H = hb[bi[0] % 2]
        P.dma("sp", H[:, :, :n], hT_d[:, :, tok0:tok0 + n].rearrange("k p t -> p k t"), [], [H], H)
        RC = RS = None
        if rope:
            RC = rc[bi[0] % 2]
            RS = rs[bi[0] % 2]
            P.dma("sp", RC[:, :n], ropec_d[:, lat0:lat0 + n], [], [RC], RC)
            P.dma("sp", RS[:, :n], ropes_d[:, lat0:lat0 + n], [], [RS], RS)
        bi[0] += 1
        return H, RC, RS

    def proj_T(H, n, col0, dst, RC, RS):
        for dc in range(2):
            for kc in range(KC):
                P.op("pe", lambda e: e.matmul(B[dc][:, :n], win[:, kc, col0 + dc * 128:col0 + (dc + 1) * 128], H[:, kc, :n],
                                              start=(kc == 0), stop=(kc == KC - 1)), [win, H], [B[dc]], pe_acc=True)
        if RC is None:
            P.op("act", lambda e: e.copy(dst[:, 0, :n], B[0][:, :n]), [B[0]], [dst.sub(0)])
            P.op("dve", lambda e: e.tensor_copy(dst[:, 1, :n], B[1][:, :n]), [B[1]], [dst.sub(1)])
            return
        P.op("dve", lambda e: e.tensor_tensor(ra[:, :n], B[0][:, :n], RC[:, :n], ALU.mult), [B[0], RC], [ra])
        P.op("dve", lambda e: e.tensor_tensor(rb[:, :n], B[1][:, :n], RS[:, :n], ALU.mult), [B[1], RS], [rb])
        P.op("pool", lambda e: e.tensor_tensor(dst[:, 0, :n], ra[:, :n], rb[:, :n], ALU.subtract), [ra, rb], [dst.sub(0)])
        P.op("dve", lambda e: e.tensor_tensor(ra[:, :n], B[1][:, :n], RC[:, :n], ALU.mult), [B[1], RC], [ra])
        P.op("dve", lambda e: e.tensor_tensor(rb[:, :n], B[0][:, :n], RS[:, :n], ALU.mult), [B[0], RS], [rb])
        P.op("pool", lambda e: e.tensor_tensor(dst[:, 1, :n], ra[:, :n], rb[:, :n], ALU.add), [ra, rb], [dst.sub(1)])

    def proj_v(H, j0, col0, bank):
        for kc in range(KC):
            P.op("pe", lambda e: e.matmul(bank[:, :], H[:, kc, j0:j0 + 128], win[:, kc, col0:col0 + 512],
                                          start=(kc == 0), stop=(kc == KC - 1)), [win, H], [bank], pe_acc=True)

    def k_tok(h, j0, need_b):
        for dc in range(2):
            P.op("pe", lambda e: e.transpose(KTP[:, dc * 128:(dc + 1) * 128], kTr[:, dc, j0:j0 + 128], ident[:]),
                 [kTr, ident], [KTP], pe_acc=True)
        P.op("act", lambda e: e.activation(out=kf[:], in_=KTP[:, :256], func=AF.Copy, scale=kd[:, h, 0:1]), [KTP, kd], [kf])
        if need_b:
            P.op("act", lambda e: e.activation(out=kb[:], in_=KTP[:, :256], func=AF.Copy, scale=kd[:, h, 1:2]), [KTP, kd], [kb])

    for h in range(4):
        cq, ck, cv, cg = h * 256, 1024 + h * 256, 2048 + h * 512, 4096 + h * 512
        segs = [("lat", NCTX, NCH_LAT, True, 2)]
        if pass2:
            segs = [("ctx", 0, 2, False, 0)] + segs
        for (sname, tbase, nch, rope, ce0) in segs:
            S_b, A_f = Sb[sname], Af[sname]
            P.op("dve", lambda e: e.memset(S_b[:], 0.0), [], [S_b])
            P.op("pool", lambda e: e.memset(A_f[:], 0.0), [], [A_f])
            P.op("pool", lambda e: e.memset(Sb_bf[:], 0.0), [], [Sb_bf])
            blocks = [(c0, min(4, nch - c0)) for c0 in range(0, nch, 4)]
            for (c0, nb) in reversed(blocks):
                n = nb * 128
                H, RC, RS = load_block(tbase + c0 * 128, n, c0 * 128, rope)
                proj_T(H, n, ck, kTr, RC, RS)
                for c in reversed(range(c0, c0 + nb)):
                    j0 = (c - c0) * 128
                    V = vb[ci[0] % 2]
                    ci[0] += 1
                    proj_v(H, j0, cv, B[2])
                    P.op("act", lambda e: e.copy(V[:], B[2][:, :]), [B[2]], [V])
                    k_tok(h, j0, True)
                    for dc in range(2):
                        P.op("pe", lambda e: e.matmul(B[3 + dc][:, :], kb[:, dc * 128:(dc + 1) * 128], V[:], start=True, stop=True),
                             [kb, V], [B[3 + dc]], pe_acc=True)
                        P.op("pe", lambda e: e.matmul(B[5 + dc][:, :], kf[:, dc * 128:(dc + 1) * 128], V[:], start=True, stop=True),
                             [kf, V], [B[5 + dc]], pe_acc=True)
                    if pass2:
                        P.dma("sp", spill_d[h, ce0 + c, :, :], Sb_bf[:].rearrange("p a b -> p (a b)"), [Sb_bf],
                              [spill_d.sub((h, ce0 + c))], Sb_bf)
                    for dc in range(2):
                        P.op("dve", lambda e: e.scalar_tensor_tensor(S_b[:, dc, :], S_b[:, dc, :], g128[:, h, 1:2], B[3 + dc][:, :],
                                                                     ALU.mult, ALU.add), [S_b, g128, B[3 + dc]], [S_b])
                        P.op("dve", lambda e: e.scalar_tensor_tensor(A_f[:, dc, :], B[5 + dc][:, :],
                                                                     wA[:, h, 0, ce0 + (nch - 1 - c):ce0 + (nch - 1 - c) + 1] if False else wA[:, h, 0, ce0 + c:ce0 + c + 1],
                                                                     A_f[:, dc, :], ALU.mult, ALU.add), [A_f, wA, B[5 + dc]], [A_f])
                    if pass2:
                        P.op("act", lambda e: e.copy(Sb_bf[:], S_b[:]), [S_b], [Sb_bf])
            if not pass2:
                base = (h * 2 + 0) * 1024
                P.dma("sp", aout_d[:, base:base + 1024], A_f[:].rearrange("p a b -> p (a b)"), [A_f], [], A_f, is_output=True)
                base = (h * 2 + 1) * 1024
                P.dma("sp", aout_d[:, base:base + 1024], S_b[:].rearrange("p a b -> p (a b)"), [S_b], [], S_b, is_output=True)
        if not pass2:
            continue
        for (sname, tbase, nch, rope, ce0) in segs:
            if sname == "ctx":
                P.op("dve", lambda e: e.memset(Sf[:], 0.0), [], [Sf])
                P.op("pool", lambda e: e.memset(Sf_bf[:], 0.0), [], [Sf_bf])
            else:
                for (dst, src_ctx, d, wo) in ((Sf, Af["ctx"], 0, 0), (Sbin, Sb["ctx"], 1, 5)):
                    P.op("dve", lambda e: e.tensor_scalar(dst[:], src_ctx[:], wc[:, h, wo + 4:wo + 5], None, ALU.mult),
                         [src_ctx, wc], [dst])
                    for s in range(4):
                        ag = astg[s % 2]
                        base = (h * 2 + d) * 1024
                        P.dma("sp", ag[:].rearrange("p a b -> p (a b)"), ain_d[s, :, base:base + 1024], [], [ag], ag)
                        P.op("dve", lambda e: e.scalar_tensor_tensor(dst[:], ag[:], wc[:, h, wo + s:wo + s + 1], dst[:],
                                                                     ALU.mult, ALU.add), [ag, wc, dst], [dst])
                P.op("act", lambda e: e.copy(Sf_bf[:], Sf[:]), [Sf], [Sf_bf])
            blocks = [(c0, min(4, nch - c0)) for c0 in range(0, nch, 4)]
            for (c0, nb) in blocks:
                n = nb * 128
                H, RC, RS = load_block(tbase + c0 * 128, n, c0 * 128, rope)
                proj_T(H, n, cq, qTr, RC, RS)
                proj_T(H, n, ck, kTr, RC, RS)
                for c in range(c0, c0 + nb):
                    j0 = (c - c0) * 128
                    V = vb[ci[0] % 2]
                    O = ob[ci[0] % 2]
                    SL = sbl[ci[0] % 2]
                    ci[0] += 1
                    P.dma("sp", SL[:].rearrange("p a b -> p (a b)"), spill_d[h, ce0 + c, :, :], [spill_d.sub((h, ce0 + c))], [SL], SL)
                    proj_v(H, j0, cv, B[2])
                    P.op("act", lambda e: e.copy(V[:], B[2][:, :]), [B[2]], [V])
                    proj_v(H, j0, cg, B[3])
                    P.op("act", lambda e: e.activation(out=sg[:], in_=B[3][:, :], func=AF.Silu), [B[3]], [sg])
                    k_tok(h, j0, False)
                    for dc in range(2):
                        P.op("pe", lambda e: e.matmul(B[4][:, :128], kTr[:, dc, j0:j0 + 128], qTr[:, dc, j0:j0 + 128],
                                                      start=(dc == 0), stop=(dc == 1)), [kTr, qTr], [B[4]], pe_acc=True)
                    P.op("dve", lambda e: e.tensor_tensor(sT[:], B[4][:, :128], DT[:, h, :], ALU.mult), [B[4], DT], [sT])
                    P.op("pool", lambda e: e.tensor_tensor(qf[:], qTr[:, :, j0:j0 + 128],
                                                           qd[:, h, 0:1, :].to_broadcast([128, 2, 128]), ALU.mult), [qTr, qd], [qf])
                    P.op("pool", lambda e: e.tensor_tensor(qb[:], qTr[:, :, j0:j0 + 128],
                                                           qd[:, h, 1:2, :].to_broadcast([128, 2, 128]), ALU.mult), [qTr, qd], [qb])
                    if sname == "ctx":
                        SBT = SL
                    else:
                        SBT = Sbt
                        P.op("dve", lambda e: e.scalar_tensor_tensor(Sbt[:], Sbin[:], wA[:, h, 1, ce0 + c:ce0 + c + 1], SL[:],
                                                                     ALU.mult, ALU.add), [Sbin, wA, SL], [Sbt])
                    P.op("pe", lambda e: e.matmul(B[5][:, :], sT[:], V[:], start=True, stop=False), [sT, V], [B[5]], pe_acc=True)
                    for dc in range(2):
                        P.op("pe", lambda e: e.matmul(B[5][:, :], qf[:, dc, :], Sf_bf[:, dc, :], start=False, stop=False),
                             [qf, Sf_bf], [B[5]], pe_acc=True)
                    for dc in range(2):
                        P.op("pe", lambda e: e.matmul(B[5][:, :], qb[:, dc, :], SBT[:, dc, :], start=False, stop=(dc == 1)),
                             [qb, SBT], [B[5]], pe_acc=True)
                    P.op("act", lambda e: e.activation(out=junk[:], in_=B[5][:, :], func=AF.Identity, accum_out=stt[:, 0:1]),
                         [B[5]], [junk, stt])
                    P.op("act", lambda e: e.activation(out=junk[:], in_=B[5][:, :], func=AF.Square, accum_out=stt[:, 1:2]),
                         [B[5]], [junk, stt])
                    P.op("dve", lambda e: e.tensor_scalar(stt[:, 2:3], stt[:, 0:1], 1.0 / 512, None, ALU.mult), [stt], [stt])
                    P.op("dve", lambda e: e.tensor_tensor(stt[:, 3:4], stt[:, 2:3], stt[:, 2:3], ALU.mult), [stt], [stt])
                    P.op("dve", lambda e: e.scalar_tensor_tensor(stt[:, 4:5], stt[:, 1:2], 1.0 / 512, stt[:, 3:4], ALU.mult,
                                                                 ALU.subtract), [stt], [stt])
                    P.op("dve", lambda e: e.tensor_scalar(stt[:, 4:5], stt[:, 4:5], EPS, None, ALU.add), [stt], [stt])
                    P.op("act", lambda e: e.activation(out=stt[:, 5:6], in_=stt[:, 4:5], func=AF.Sqrt), [stt], [stt])
                    P.op("dve", lambda e: e.reciprocal(stt[:, 6:7], stt[:, 5:6]), [stt], [stt])
                    P.op("dve", lambda e: e.scalar_tensor_tensor(stt[:, 7:8], stt[:, 2:3], -1.0, stt[:, 6:7], ALU.mult, ALU.mult),
                         [stt], [stt])
                    P.op("act", lambda e: e.activation(out=yn[:], in_=B[5][:, :], func=AF.Identity, scale=stt[:, 6:7],
                                                        bias=stt[:, 7:8]), [B[5], stt], [yn])
                    P.op("pool", lambda e: e.tensor_tensor(O[:], yn[:], sg[:], ALU.mult), [yn, sg], [O])
                    t0 = tbase + c * 128
                    P.dma("sp", o_d[t0:t0 + 128, h * 512:(h + 1) * 512], O[:], [O], [], O, is_output=True)
                    P.op("pe", lambda e: e.matmul(B[6][:, :], kf[:, 0:128], V[:], start=True, stop=True), [kf, V], [B[6]], pe_acc=True)
                    P.op("pe", lambda e: e.matmul(B[4][:, :], kf[:, 128:256], V[:], start=True, stop=True), [kf, V], [B[4]], pe_acc=True)
                    for dc, bk in ((0, B[6]), (1, B[4])):
                        P.op("dve", lambda e: e.scalar_tensor_tensor(Sf[:, dc, :], Sf[:, dc, :], g128[:, h, 0:1], bk[:, :],
                                                                     ALU.mult, ALU.add), [Sf, g128, bk], [Sf])
                    P.op("act", lambda e: e.copy(Sf_bf[:], Sf[:]), [Sf], [Sf_bf])
    P.finish()
    P.close()
    return nc


def ret_consts(NCH_LAT, seg_j, lat_pos0):
    j = np.arange(128)[:, None]
    i = np.arange(128)[None, :]
    posU = np.maximum(i - j, 0).astype(np.float32)
    posL = np.maximum(j - i, 0).astype(np.float32)
    U = (i > j).astype(np.float32)
    L = (i < j).astype(np.float32)
    I2 = 2.0 * (i == j).astype(np.float32)
    cmask = np.stack([posU, posL, U, L, I2], 1).reshape(128, 5 * 128)
    qrow = np.stack([np.arange(128) + 1.0, 128.0 - np.arange(128)]).astype(np.float32)
    kcol = np.stack([127.0 - np.arange(128), np.arange(128) * 1.0], 1).astype(np.float32)
    cexp = np.concatenate([128.0 * (1 - np.arange(2)), 128.0 * (NCH_LAT - 1 - np.arange(NCH_LAT))])[None, :].astype(np.float32)
    Lseg = 128.0 * NCH_LAT
    ex = np.zeros(10, np.float32)
    mk = np.zeros(10, np.float32)
    for s in range(4):
        if s < seg_j:
            ex[s] = Lseg * (seg_j - 1 - s); mk[s] = 1
        if s > seg_j:
            ex[5 + s] = Lseg * (s - seg_j - 1); mk[5 + s] = 1
    ex[4] = Lseg * seg_j; mk[4] = 1
    ex[9] = Lseg * (3 - seg_j); mk[9] = 1
    comb = np.concatenate([ex, mk])[None, :].astype(np.float32)
    n = np.arange(lat_pos0, lat_pos0 + 128 * NCH_LAT)
    row = (n // 64).astype(np.float32)
    col = (n % 64).astype(np.float32)
    inv = (np.float32(10000.0) ** (-(np.arange(64, dtype=np.float32)) / np.float32(64))).astype(np.float32)
    ang = np.concatenate([row[:, None] * inv, col[:, None] * inv], -1).astype(np.float32)
    ropec = np.ascontiguousarray(np.cos(ang).T).astype(np.float32)
    ropes = np.ascontiguousarray(np.sin(ang).T).astype(np.float32)
    return dict(cmask=cmask, qrow=qrow, kcol=kcol, cexp=cexp, comb=comb, ropec=ropec, ropes=ropes)


NCORES = 8
SEQ = 16384
NLATC = 4096
NCTXC = 64
NTC = NLATC + NCTXC
_BF = ml_dtypes.bfloat16
_PROGS = {}


def _prog(key, fn):
    if key not in _PROGS:
        _PROGS[key] = fn()
    return _PROGS[key]


def _run(nc, in_maps):
    res = run_bass_kernel_spmd(nc, in_maps, core_ids=list(range(NCORES)))
    return res.results


def _ppl(v):
    return np.ascontiguousarray(np.asarray(v, np.float32).reshape(8, 128).T)


def kernel(x, c, ctx, c_ctx, ada_w, ada_b, norm_g, final_g, ffn_w_gate, ffn_w_up, ffn_w_down,
           four_w, four_b, ret_w_in, ret_w_out, ret_decay):
    f32 = np.float32
    x = np.asarray(x, f32); ctx = np.asarray(ctx, f32)
    ident = np.eye(128).astype(_BF)
    zrow = np.zeros(1024, f32)
    X = []
    for i in range(NCORES):
        b, j = divmod(i, 4)
        X.append(np.ascontiguousarray(np.concatenate([x[b, j * NLATC:(j + 1) * NLATC], ctx[b, j * NCTXC:(j + 1) * NCTXC]], 0)))
    W = np.ascontiguousarray(np.asarray(ada_w, f32).transpose(1, 0, 2).reshape(1024, 4 * 9216))
    Bv = np.asarray(ada_b, f32).reshape(1, 4 * 9216)
    cin = np.stack([np.asarray(c, f32)[0], np.asarray(c, f32)[1], np.asarray(c_ctx, f32)])
    cl = np.ascontiguousarray(cin.reshape(3, 8, 128).transpose(2, 1, 0).reshape(128, 24))
    ncm = _prog("mod", build_mod)
    rm = _run(ncm, [dict(w=np.ascontiguousarray(W[:, i * NCOL:(i + 1) * NCOL]), b=np.ascontiguousarray(Bv[:, i * NCOL:(i + 1) * NCOL]), c=cl)
                    for i in range(NCORES)])
    M = np.concatenate([r["m"] for r in rm], 1).reshape(3, 4, 9, 1024)
    del W

    def tok_inputs(l, i, g_ffn, ffn_idx, g_out, out_idx, extra):
        b = i // 4
        ml, mc = M[b, l], M[2, l]
        z = zrow
        vecs = [g_ffn if g_ffn is not None else z]
        vecs += [ml[ffn_idx], ml[ffn_idx + 1], mc[ffn_idx], mc[ffn_idx + 1]] if ffn_idx is not None else [z, z, z, z]
        vecs += [g_out if g_out is not None else z]
        vecs += [ml[out_idx], ml[out_idx + 1], mc[out_idx], mc[out_idx + 1]] if out_idx is not None else [z, z, z, z]
        pp = np.ascontiguousarray(np.stack([_ppl(v) for v in vecs], 1))
        rows = np.stack([ml[ffn_idx + 2] if ffn_idx is not None else z, mc[ffn_idx + 2] if ffn_idx is not None else z,
                         ml[5], mc[5], extra.get("bmix", z), np.asarray(final_g, f32)]).astype(f32)
        d = dict(x=X[i], ident=ident, pp=pp, rows=np.ascontiguousarray(rows))
        return d

    cs = None
    fconst = None
    for l in range(4):
        mixer = l % 2
        slot = l // 2
        mode = "pq" if mixer == 0 else "hT"
        nca = _prog("tokA_" + mode, lambda: build_tok(NLATC, NCTXC, ffn=True, mix_k=0, norm_out=mode))
        wg = np.ascontiguousarray(np.asarray(ffn_w_gate, f32)[l, 0]); wu = np.ascontiguousarray(np.asarray(ffn_w_up, f32)[l, 0])
        wd = np.ascontiguousarray(np.asarray(ffn_w_down, f32)[l, 0])
        ins = []
        for i in range(NCORES):
            d = tok_inputs(l, i, np.asarray(norm_g, f32)[l, 0], 0, np.asarray(norm_g, f32)[l, 1], 3, {})
            d.update(wg=wg, wu=wu, wd=wd)
            if mode == "pq":
                if cs is None:
                    mm = np.arange(256)
                    ang = 2 * np.pi * ((np.outer(mm, mm)) % 256) / 256.0
                    csf = np.concatenate([np.cos(ang), np.sin(ang)], 1).reshape(2, 128, 512).transpose(1, 0, 2)
                    cs = np.ascontiguousarray(csf).astype(_BF)
                d["cs"] = cs
            ins.append(d)
        ra = _run(nca, ins)
        X = [r["y"] for r in ra]
        if mixer == 0:
            if fconst is None:
                fconst = four_consts()
            ncf = _prog("four", build_four)
            ins = []
            for i in range(NCORES):
                b, g = divmod(i, 4)
                lat = np.concatenate([ra[4 * b + j]["pq"][:NLATC, g, :] for j in range(4)], 0)
                cx = np.concatenate([ra[4 * b + j]["pq"][NLATC:, g, :] for j in range(4)], 0)
                z_, zc_ = four_layout_in(lat, cx)
                ins.append(dict(z=z_, zc=zc_, **fconst))
            rf = _run(ncf, ins)
            U = []
            fl = {}
            for i in range(NCORES):
                fl[i] = four_layout_out(rf[i]["r"], rf[i]["rc"])
            for i in range(NCORES):
                b, j = divmod(i, 4)
                lat = np.concatenate([fl[4 * b + g][0][j * NLATC:(j + 1) * NLATC] for g in range(4)], 1)
                cx = np.concatenate([fl[4 * b + g][1][j * NCTXC:(j + 1) * NCTXC] for g in range(4)], 1)
                U.append(np.ascontiguousarray(np.concatenate([lat, cx], 0).astype(f32)))
            mk = 1024
            wm = np.ascontiguousarray(np.asarray(four_w, f32)[slot])
            bmix = np.asarray(four_b, f32)[slot]
        else:
            nr1 = _prog("ret1", lambda: build_ret(NLATC // 128, False))
            nr2 = _prog("ret2", lambda: build_ret(NLATC // 128, True))
            win = np.ascontiguousarray(np.asarray(ret_w_in, f32)[slot])
            dec = np.ascontiguousarray(np.asarray(ret_decay, f32)[slot].reshape(1, 8))
            ins = []
            for i in range(NCORES):
                b, j = divmod(i, 4)
                hc = np.concatenate([ra[4 * b + jj]["hT"][:, :, NLATC:] for jj in range(4)], 2)
                hT = np.ascontiguousarray(np.concatenate([hc, ra[i]["hT"][:, :, :NLATC]], 2))
                d = dict(hT=hT, win=win, dec=dec, ident=ident)
                d.update(ret_consts(NLATC // 128, j, j * NLATC))
                ins.append(d)
            r1 = _run(nr1, ins)
            for i in range(NCORES):
                b = i // 4
                ins[i]["ain"] = np.ascontiguousarray(np.stack([r1[4 * b + s]["aout"] for s in range(4)]))
            r2 = _run(nr2, ins)
            U = []
            for i in range(NCORES):
                j = i % 4
                o = r2[i]["o"]
                U.append(np.ascontiguousarray(np.concatenate([o[256:], o[j * NCTXC:(j + 1) * NCTXC]], 0)))
            mk = 2048
            wm = np.ascontiguousarray(np.asarray(ret_w_out, f32)[slot])
            bmix = zrow
        ncb = _prog("tokB_%d" % mk, lambda: build_tok(NLATC, NCTXC, ffn=False, mix_k=mk, norm_out=None))
        ins = []
        for i in range(NCORES):
            d = tok_inputs(l, i, None, None, None, None, dict(bmix=bmix))
            d.update(u=U[i], wm=wm)
            ins.append(d)
        rb = _run(ncb, ins)
        X = [r["y"] for r in rb]
        ncc = _prog("tokC", lambda: build_tok(NLATC, NCTXC, ffn=True, mix_k=0, norm_out=None))
        wg = np.ascontiguousarray(np.asarray(ffn_w_gate, f32)[l, 1]); wu = np.ascontiguousarray(np.asarray(ffn_w_up, f32)[l, 1])
        wd = np.ascontiguousarray(np.asarray(ffn_w_down, f32)[l, 1])
        ins = []
        for i in range(NCORES):
            d = tok_inputs(l, i, np.asarray(norm_g, f32)[l, 2], 6, None, None, {})
            d.update(wg=wg, wu=wu, wd=wd)
            ins.append(d)
        rc_ = _run(ncc, ins)
        X = [r["y"] for r in rc_]
    ncz = _prog("final", lambda: build_tok(NLATC, NCTXC, ffn=False, mix_k=0, norm_out="final"))
    ins = [tok_inputs(3, i, None, None, None, None, {}) for i in range(NCORES)]
    rz = _run(ncz, ins)
    out = np.empty((2, SEQ, 1024), f32)
    for i in range(NCORES):
        b, j = divmod(i, 4)
        out[b, j * NLATC:(j + 1) * NLATC] = rz[i]["yf"][:NLATC]
    return out
```

```python
import numpy as np
import ml_dtypes
from concourse.bass_utils import run_bass_kernel_spmd
import concourse.bass as bass
import concourse.mybir as mybir

F32 = mybir.dt.float32
BF16 = mybir.dt.bfloat16
AF = mybir.ActivationFunctionType
ALU = mybir.AluOpType
AX = mybir.AxisListType


class Region:
    __slots__ = ("tile", "sub")

    def __init__(self, tile, sub):
        self.tile = tile
        self.sub = sub


class Tile:
    def __init__(self, prog, name, t, space):
        self.prog = prog
        self.name = name
        self.t = t
        self.space = space
        self.whole_writer = None
        self.sub_writers = {}
        self.whole_readers = []
        self.sub_readers = {}
        self.dma_sems = {}

    def __getitem__(self, k):
        return self.t[k]

    def sub(self, k):
        return Region(self, k)


class Prog:
    SEM_LIMIT = 30000

    def __init__(self, nc):
        self.nc = nc
        self.eng = {"pe": nc.tensor, "act": nc.scalar, "dve": nc.vector, "pool": nc.gpsimd, "sp": nc.sync}
        self.sem = {}
        self.cnt = {}
        self.known = {k: {} for k in self.eng}
        self.semobjs = {}
        self.nsem = 0
        self.ctxs = []
        self.out_events = []
        for k in self.eng:
            self._new_engine_sem(k)

    def _alloc_sem(self, name):
        cm = self.nc.semaphore(name)
        s = cm.__enter__()
        self.ctxs.append(cm)
        self.nsem += 1
        sid = self.nsem
        self.semobjs[sid] = s
        return sid

    def _new_engine_sem(self, k):
        self.sem[k] = self._alloc_sem(f"s_{k}_{self.nsem}")
        self.cnt[k] = 0

    def sbuf(self, name, shape, dt):
        cm = self.nc.sbuf_tensor(name, list(shape), dt)
        t = cm.__enter__()
        self.ctxs.append(cm)
        return Tile(self, name, t, "sbuf")

    def psum(self, name, shape, dt):
        cm = self.nc.psum_tensor(name, list(shape), dt)
        t = cm.__enter__()
        self.ctxs.append(cm)
        return Tile(self, name, t, "psum")

    def dram(self, name, shape, dt, kind):
        t = self.nc.dram_tensor(name, list(shape), dt, kind=kind).ap()
        return Tile(self, name, t, "dram")

    def close(self):
        for cm in reversed(self.ctxs):
            cm.__exit__(None, None, None)
        self.ctxs = []

    @staticmethod
    def _reg(r):
        if isinstance(r, Tile):
            return r, None
        return r.tile, r.sub

    def _deps(self, reads, writes):
        deps = []
        for r in reads:
            t, s = self._reg(r)
            if t.whole_writer is not None:
                deps.append(t.whole_writer)
            if s is None:
                deps.extend(t.sub_writers.values())
            elif s in t.sub_writers:
                deps.append(t.sub_writers[s])
        for w in writes:
            t, s = self._reg(w)
            if t.whole_writer is not None:
                deps.append(t.whole_writer)
            deps.extend(t.whole_readers)
            if s is None:
                deps.extend(t.sub_writers.values())
                for l in t.sub_readers.values():
                    deps.extend(l)
            else:
                if s in t.sub_writers:
                    deps.append(t.sub_writers[s])
                deps.extend(t.sub_readers.get(s, ()))
        return deps

    def _record(self, ev, reads, writes):
        for r in reads:
            t, s = self._reg(r)
            if s is None:
                t.whole_readers.append(ev)
            else:
                t.sub_readers.setdefault(s, []).append(ev)
        for w in writes:
            t, s = self._reg(w)
            if s is None:
                t.whole_writer = ev
                t.sub_writers = {}
                t.whole_readers = []
                t.sub_readers = {}
            else:
                t.sub_writers[s] = ev
                t.sub_readers[s] = []

    def _emit_waits(self, k, deps, skip_pe_acc=False):
        e = self.eng[k]
        need = {}
        for ev in deps:
            kind, sid, val, src = ev
            if kind == "dma":
                val = src[1] * 16
            elif skip_pe_acc and src == "pe":
                continue
            if need.get(sid, 0) < val:
                need[sid] = val
        kn = self.known[k]
        for sid, val in need.items():
            if kn.get(sid, 0) >= val:
                continue
            e.wait_ge(self.semobjs[sid], val)
            kn[sid] = val

    def op(self, k, fn, reads=(), writes=(), pe_acc=False):
        deps = self._deps(reads, writes)
        self._emit_waits(k, deps, skip_pe_acc=pe_acc)
        if self.cnt[k] >= self.SEM_LIMIT:
            self._new_engine_sem(k)
        ins = fn(self.eng[k])
        self.cnt[k] += 1
        sid = self.sem[k]
        ins.then_inc(self.semobjs[sid], 1)
        ev = ("c", sid, self.cnt[k], k)
        self._record(ev, reads, writes)
        return ev

    def dma(self, q, out_ap, in_ap, reads, writes, sem_tile, is_output=False, sem_key=None, **kw):
        deps = self._deps(reads, writes)
        self._emit_waits(q, deps)
        rec = sem_tile.dma_sems.get(sem_key)
        if rec is None:
            rec = [self._alloc_sem(f"d_{sem_tile.name}_{len(sem_tile.dma_sems)}"), 0]
            sem_tile.dma_sems[sem_key] = rec
        ins = self.eng[q].dma_start(out=out_ap, in_=in_ap, **kw)
        ins.then_inc(self.semobjs[rec[0]], 16)
        rec[1] += 1
        ev = ("dma", rec[0], rec[1] * 16, rec)
        self._record(ev, reads, writes)
        if is_output:
            self.out_events.append(ev)
        return ev

    def finish(self, q="sp"):
        self._emit_waits(q, self.out_events)


D = 1024
F = 2816
KC = 8
FC = 22
EPS = 1e-6


def cast_op(P, i, out_ap, in_ap, reads, writes, psum=False):
    k = ("act", "dve")[i % 2] if psum else ("act", "dve", "pool")[i % 3]

    if k == "act":
        return P.op("act", lambda e: e.copy(out_ap, in_ap), reads, writes)
    return P.op(k, lambda e: e.tensor_copy(out_ap, in_ap), reads, writes)


def load_cast_weight(P, w, dst, K, N, stages, ctr, scale=None):
    pieces = [(o, min(512, N - o)) for o in range(0, N, 512)]
    for kc in range(K // 128):
        for (o, n) in pieces:
            st = stages[ctr[0] % len(stages)]
            P.dma("sp", st[:, :n], w[kc * 128:(kc + 1) * 128, o:o + n], reads=[], writes=[st], sem_tile=st)
            if scale is None:
                cast_op(P, ctr[0], dst[:, kc, o:o + n], st[:, :n], [st], [dst.sub(("w", kc, o))])
            else:
                P.op("dve", lambda e: e.tensor_scalar(dst[:, kc, o:o + n], st[:, :n], scale, None, ALU.mult),
                     [st], [dst.sub(("w", kc, o))])
            ctr[0] += 1


class _Stop(Exception):
    pass


def build_tok(NT_lat, NT_ctx, ffn=True, mix_k=0, norm_out=None, T=256, stop=0):
    nc = bass.Bass("TRN2", target_bir_lowering=False)
    P = Prog(nc)
    NT = NT_lat + NT_ctx
    x = P.dram("x", [NT, D], F32, "ExternalInput")
    ident_d = P.dram("ident", [128, 128], BF16, "ExternalInput")
    pp_d = P.dram("pp", [128, 10, KC], F32, "ExternalInput")
    rows_d = P.dram("rows", [6, D], F32, "ExternalInput")
    if norm_out != "final":
        y = P.dram("y", [NT, D], F32, "ExternalOutput")
    if ffn:
        wg_d = P.dram("wg", [D, F], F32, "ExternalInput")
        wu_d = P.dram("wu", [D, F], F32, "ExternalInput")
        wd_d = P.dram("wd", [F, D], F32, "ExternalInput")
    if mix_k:
        u_d = P.dram("u", [NT, mix_k], BF16 if mix_k == 2048 else F32, "ExternalInput")
        wm_d = P.dram("wm", [mix_k, D], F32, "ExternalInput")
    if norm_out == "hT":
        hT_d = P.dram("hT", [KC, 128, NT], BF16, "ExternalOutput")
    if norm_out == "pq":
        cs_d = P.dram("cs", [128, 2, 512], BF16, "ExternalInput")
        pq_d = P.dram("pq", [NT, 4, 512], BF16, "ExternalOutput")
    if norm_out == "final":
        yf = P.dram("yf", [NT, D], F32, "ExternalOutput")

    ident = P.sbuf("identb", [128, 128], BF16)
    P.dma("sp", ident[:], ident_d[:], [], [ident], ident)
    pp = P.sbuf("ppv", [128, 10, KC], F32)
    P.dma("sp", pp[:], pp_d[:], [], [pp], pp)
    gsv = P.sbuf("gsv", [128, 4, KC], F32)
    for (dst, gi, si) in ((0, 0, 2), (1, 0, 4), (2, 5, 7), (3, 5, 9)):
        P.op("dve", lambda e: e.tensor_tensor(gsv[:, dst, :], pp[:, gi, :], pp[:, si, :], ALU.mult), [pp], [gsv.sub(dst)])
        P.op("dve", lambda e: e.tensor_tensor(gsv[:, dst, :], gsv[:, dst, :], pp[:, gi, :], ALU.add), [pp, gsv.sub(dst)], [gsv.sub(dst)])
    need_rows = ([0, 1] if ffn else []) + ([2, 3, 4] if mix_k else []) + ([5] if norm_out == "final" else [])
    RS = {r: i for i, r in enumerate(need_rows)}
    rows = P.sbuf("rowsb", [128, max(1, len(need_rows)), D], F32)
    for r in need_rows:
        P.dma("sp", rows[:, RS[r], :], rows_d[r:r + 1, :].partition_broadcast(128), [], [rows.sub(RS[r])], rows)
    if ffn:
        for r in (0, 1):
            P.op("pool", lambda e: e.tensor_scalar(rows[:, RS[r], :], rows[:, RS[r], :], 0.5, None, ALU.mult), [rows.sub(RS[r])], [rows.sub(RS[r])])

    stages = [P.sbuf(f"stg{i}", [128, 512], F32) for i in range(4)]
    ctr = [0]
    if ffn:
        wg = P.sbuf("wg_b", [128, KC, F], BF16)
        wu = P.sbuf("wu_b", [128, KC, F], BF16)
        wd = P.sbuf("wd_b", [128, FC, D], BF16)
        load_cast_weight(P, wg_d, wg, D, F, stages, ctr)
        load_cast_weight(P, wu_d, wu, D, F, stages, ctr)
        load_cast_weight(P, wd_d, wd, F, D, stages, ctr)
    if mix_k:
        MK = mix_k // 128
        wm = P.sbuf("wm_b", [128, MK, D], BF16)
        load_cast_weight(P, wm_d, wm, mix_k, D, stages, ctr)
    if norm_out == "pq":
        cs = P.sbuf("csb", [128, 2, 512], BF16)
        P.dma("sp", cs[:], cs_d[:], [], [cs], cs)

    if stop == 1:
        P.dma("sp", y[0:128, :], rows[:, 0, :], [rows], [], rows, is_output=True)
        P.finish(); P.close(); return nc
    NTL = T // 128
    xt = [P.sbuf(f"xt{i}", [128, NTL, D], F32) for i in range(2)]
    xn = P.sbuf("xn", [128, NTL, D], BF16)
    st = P.sbuf("st", [128, 4 * NTL], F32)
    psT = P.psum("psT", [128, KC, T], BF16)
    hT = P.sbuf("hT_s", [128, KC, T], BF16)
    if ffn:
        pgu = [P.psum(f"pgu{i}", [128, 512], F32) for i in range(4)]
        sig = [P.sbuf(f"sig{i}", [128, T], F32) for i in range(2)]
        actT = P.sbuf("actT", [128, FC, T], BF16)
    po = [P.psum(f"po{i}", [128, 512], F32) for i in range(2)]
    dtmp = [P.sbuf(f"dtmp{i}", [128, 512], F32) for i in range(2)]
    if mix_k:
        ut = P.sbuf("ut", [128, NTL, mix_k], BF16 if mix_k == 2048 else F32)
        ub = P.sbuf("ub", [128, NTL, mix_k], BF16) if mix_k != 2048 else ut
        uT = P.sbuf("uT", [128, MK, T], BF16)
    if norm_out == "pq":
        pqs = P.sbuf("pqs", [128, 4, 512], BF16)
    if norm_out == "final":
        yo = P.sbuf("yo", [128, NTL, D], F32)
    rr = [0]

    def norm_T(X, tiles, TT, gsi, shi, dstT):
        for ti, (o, n) in enumerate(tiles):
            P.op("act", lambda e: e.activation(out=xn[:n, ti, :], in_=X[:n, ti, :], func=AF.Square,
                                                accum_out=st[:n, 4 * ti:4 * ti + 1]),
                 [X.sub(ti)], [xn.sub(ti), st.sub(ti)])
            P.op("dve", lambda e: e.tensor_scalar(st[:n, 4 * ti + 1:4 * ti + 2], st[:n, 4 * ti:4 * ti + 1],
                                                   1.0 / D, EPS, ALU.mult, ALU.add), [st.sub(ti)], [st.sub(ti)])
            P.op("act", lambda e: e.activation(out=st[:n, 4 * ti + 2:4 * ti + 3], in_=st[:n, 4 * ti + 1:4 * ti + 2],
                                                func=AF.Sqrt), [st.sub(ti)], [st.sub(ti)])
            P.op("dve", lambda e: e.reciprocal(st[:n, 4 * ti + 3:4 * ti + 4], st[:n, 4 * ti + 2:4 * ti + 3]),
                 [st.sub(ti)], [st.sub(ti)])
            P.op("dve", lambda e: e.tensor_scalar(xn[:n, ti, :], X[:n, ti, :], st[:n, 4 * ti + 3:4 * ti + 4], None,
                                                   ALU.mult), [X.sub(ti), st.sub(ti)], [xn.sub(ti)])
            for kc in range(KC):
                P.op("pe", lambda e: e.transpose(psT[:, kc, o:o + n], xn[:n, ti, kc * 128:(kc + 1) * 128],
                                                 ident[:n, :n]),
                     [xn.sub(ti), ident], [psT.sub(kc // 4)], pe_acc=True)
        for kc in range(KC):
            rds = [psT.sub(kc // 4)]
            if gsi is None:
                cast_op(P, kc // 4, dstT[:, kc, :TT], psT[:, kc, :TT], rds, [dstT.sub(kc)], psum=True)
            elif kc // 4 == 0:
                P.op("act", lambda e: e.activation(out=dstT[:, kc, :TT], in_=psT[:, kc, :TT], func=AF.Identity,
                                                    scale=gsv[:, gsi, kc:kc + 1], bias=pp[:, shi, kc:kc + 1]),
                     rds + [gsv, pp], [dstT.sub(kc)])
            else:
                P.op("dve", lambda e: e.tensor_scalar(dstT[:, kc, :TT], psT[:, kc, :TT], gsv[:, gsi, kc:kc + 1],
                                                       pp[:, shi, kc:kc + 1], ALU.mult, ALU.add),
                     rds + [gsv, pp], [dstT.sub(kc)])

    groups = [(g0, min(T, NT_lat - g0), 0) for g0 in range(0, NT_lat, T)]
    groups += [(NT_lat + g0, min(T, NT_ctx - g0), 1) for g0 in range(0, NT_ctx, T)]
    for gi, (tok0, TT, isctx) in enumerate(groups):
        X = xt[gi % 2]
        tiles = [(o, min(128, TT - o)) for o in range(0, TT, 128)]
        for ti, (o, n) in enumerate(tiles):
            P.dma("sp", X[:n, ti, :], x[tok0 + o:tok0 + o + n, :], [], [X.sub(ti)], X, sem_key=ti)
        if mix_k:
            for ti, (o, n) in enumerate(tiles):
                P.dma("sp", ut[:n, ti, :], u_d[tok0 + o:tok0 + o + n, :], [], [ut.sub(ti)], ut, sem_key=ti)
                if ub is not ut:
                    cast_op(P, ti, ub[:n, ti, :], ut[:n, ti, :], [ut.sub(ti)], [ub.sub(ti)])
                if stop == 3:
                    P.dma("sp", y[0:128, :], X[:, 0, :], [X, ub], [], X, is_output=True, sem_key=0)
                    P.finish(); P.close(); return nc
                for k0 in range(0, MK, KC):
                    for kk in range(KC):
                        kc = k0 + kk
                        P.op("pe", lambda e: e.transpose(psT[:, kk, o:o + n], ub[:n, ti, kc * 128:(kc + 1) * 128],
                                                         ident[:n, :n]),
                             [ub.sub(ti), ident], [psT.sub(kk // 4)], pe_acc=True)
                    if stop == 4:
                        P.dma("sp", y[0:128, :], X[:, 0, :], [X, psT], [], X, is_output=True, sem_key=0)
                        P.finish(); P.close(); return nc
                    for kk in range(KC):
                        kc = k0 + kk
                        cast_op(P, kk // 4, uT[:, kc, o:o + n], psT[:, kk, o:o + n], [psT.sub(kk // 4)], [uT.sub((kc, ti))], psum=True)
            if stop == 2:
                P.dma("sp", y[0:128, :], X[:, 0, :], [X, uT], [], X, is_output=True, sem_key=0)
                P.finish(); P.close(); return nc
            for ti, (o, n) in enumerate(tiles):
                for half in range(2):
                    pb = po[rr[0] % 2]
                    dt_ = dtmp[rr[0] % 2]
                    rr[0] += 1
                    for kc in range(MK):
                        P.op("pe", lambda e: e.matmul(pb[:n, :], uT[:, kc, o:o + n], wm[:, kc, half * 512:(half + 1) * 512],
                                                      start=(kc == 0), stop=(kc == MK - 1)),
                             [uT.sub((kc, ti)), wm], [pb], pe_acc=True)
                    hs = slice(half * 512, (half + 1) * 512)
                    P.op("dve", lambda e: e.tensor_tensor(dt_[:n, :], pb[:n, :], rows[:n, RS[4], hs], ALU.add),
                         [pb, rows.sub(RS[4])], [dt_])
                    P.op("pool", lambda e: e.tensor_tensor(dt_[:n, :], dt_[:n, :], rows[:n, RS[2 + isctx], hs], ALU.mult),
                         [dt_, rows.sub(RS[2 + isctx])], [dt_])
                    P.op("pool", lambda e: e.tensor_tensor(X[:n, ti, hs], X[:n, ti, hs], dt_[:n, :], ALU.add),
                         [dt_, X.sub(ti)], [X.sub(ti)])
        if ffn:
            norm_T(X, tiles, TT, 0 + isctx, 1 + 2 * isctx, hT)
            for fc in range(FC):
                pg = pgu[2 * (fc % 2)]
                pu = pgu[2 * (fc % 2) + 1]
                sg = sig[fc % 2]
                for kc in range(KC):
                    P.op("pe", lambda e: e.matmul(pg[:, :TT], wg[:, kc, fc * 128:(fc + 1) * 128], hT[:, kc, :TT],
                                                  start=(kc == 0), stop=(kc == KC - 1)),
                         [wg, hT.sub(kc)], [pg], pe_acc=True)
                for kc in range(KC):
                    P.op("pe", lambda e: e.matmul(pu[:, :TT], wu[:, kc, fc * 128:(fc + 1) * 128], hT[:, kc, :TT],
                                                  start=(kc == 0), stop=(kc == KC - 1)),
                         [wu, hT.sub(kc)], [pu], pe_acc=True)
                P.op("act", lambda e: e.activation(out=sg[:, :TT], in_=pg[:, :TT], func=AF.Silu), [pg], [sg])
                P.op("dve", lambda e: e.tensor_tensor(actT[:, fc, :TT], pu[:, :TT], sg[:, :TT], ALU.mult),
                     [pu, sg], [actT.sub(fc)])
            for ti, (o, n) in enumerate(tiles):
                for half in range(2):
                    pb = po[rr[0] % 2]
                    dt_ = dtmp[rr[0] % 2]
                    rr[0] += 1
                    hs = slice(half * 512, (half + 1) * 512)
                    for fc in range(FC):
                        P.op("pe", lambda e: e.matmul(pb[:n, :], actT[:, fc, o:o + n], wd[:, fc, hs],
                                                      start=(fc == 0), stop=(fc == FC - 1)),
                             [actT.sub(fc), wd], [pb], pe_acc=True)
                    P.op("dve", lambda e: e.tensor_tensor(dt_[:n, :], pb[:n, :], rows[:n, RS[isctx], hs], ALU.mult),
                         [pb, rows.sub(RS[isctx])], [dt_])
                    P.op("pool", lambda e: e.tensor_tensor(X[:n, ti, hs], X[:n, ti, hs], dt_[:n, :], ALU.add),
                         [dt_, X.sub(ti)], [X.sub(ti)])
        if norm_out != "final":
            for ti, (o, n) in enumerate(tiles):
                P.dma("sp", y[tok0 + o:tok0 + o + n, :], X[:n, ti, :], [X.sub(ti)], [], X, is_output=True, sem_key=ti)
        if norm_out in ("hT", "pq"):
            norm_T(X, tiles, TT, 2 + isctx, 6 + 2 * isctx, hT)
        if norm_out == "hT":
            for kc in range(KC):
                P.dma("sp", hT_d[kc, :, tok0:tok0 + TT], hT[:, kc, :TT], [hT.sub(kc)], [], hT, is_output=True, sem_key=kc)
        if norm_out == "pq":
            for ti, (o, n) in enumerate(tiles):
                for g in range(4):
                    pb = po[rr[0] % 2]
                    rr[0] += 1
                    for c2 in range(2):
                        kc = g * 2 + c2
                        P.op("pe", lambda e: e.matmul(pb[:n, :], hT[:, kc, o:o + n], cs[:, c2, :],
                                                      start=(c2 == 0), stop=(c2 == 1)),
                             [hT.sub(kc), cs], [pb], pe_acc=True)
                    cast_op(P, g, pqs[:n, g, :], pb[:n, :], [pb], [pqs.sub(g)], psum=True)
                P.dma("sp", pq_d[tok0 + o:tok0 + o + n, :, :], pqs[:n, :, :], [pqs], [], pqs, is_output=True)
        if norm_out == "final":
            for ti, (o, n) in enumerate(tiles):
                P.op("act", lambda e: e.activation(out=yo[:n, ti, :], in_=X[:n, ti, :], func=AF.Square,
                                                    accum_out=st[:n, 4 * ti:4 * ti + 1]),
                     [X.sub(ti)], [yo.sub(ti), st.sub(ti)])
                P.op("dve", lambda e: e.tensor_scalar(st[:n, 4 * ti + 1:4 * ti + 2], st[:n, 4 * ti:4 * ti + 1],
                                                       1.0 / D, EPS, ALU.mult, ALU.add), [st.sub(ti)], [st.sub(ti)])
                P.op("act", lambda e: e.activation(out=st[:n, 4 * ti + 2:4 * ti + 3], in_=st[:n, 4 * ti + 1:4 * ti + 2],
                                                    func=AF.Sqrt), [st.sub(ti)], [st.sub(ti)])
                P.op("dve", lambda e: e.reciprocal(st[:n, 4 * ti + 3:4 * ti + 4], st[:n, 4 * ti + 2:4 * ti + 3]),
                     [st.sub(ti)], [st.sub(ti)])
                P.op("dve", lambda e: e.scalar_tensor_tensor(yo[:n, ti, :], X[:n, ti, :], st[:n, 4 * ti + 3:4 * ti + 4],
                                                              rows[:n, RS[5], :], ALU.mult, ALU.mult),
                     [X.sub(ti), st.sub(ti), rows.sub(RS[5])], [yo.sub(ti)])
                P.dma("sp", yf[tok0 + o:tok0 + o + n, :], yo[:n, ti, :], [yo.sub(ti)], [], yo, is_output=True, sem_key=ti)
    P.finish()
    P.close()
    return nc


MCH = 64
NMC = 256 // MCH


def build_four():
    nc = bass.Bass("TRN2", target_bir_lowering=False)
    P = Prog(nc)
    z_d = P.dram("z", [NMC, 128, 2 * 128 * MCH], BF16, "ExternalInput")
    zc_d = P.dram("zc", [128, 2 * 2 * 256], BF16, "ExternalInput")
    dft_d = P.dram("dft", [128, 2 * 256], BF16, "ExternalInput")
    tw_d = P.dram("tw", [128, 2 * 128 * 128], BF16, "ExternalInput")
    c256_d = P.dram("c256", [128, 2 * 2 * 256], BF16, "ExternalInput")
    r_d = P.dram("r", [NMC, 128, 128 * MCH], F32, "ExternalOutput")
    rc_d = P.dram("rc", [128, 2 * 256], F32, "ExternalOutput")

    dft = P.sbuf("dft_s", [128, 2, 256], BF16)
    P.dma("sp", dft[:], dft_d[:].rearrange("p (s k) -> p s k", s=2), [], [dft], dft)
    tw = P.sbuf("tw_s", [128, 2, 128, 128], BF16)
    for s in range(2):
        for q in range(4):
            P.dma("sp", tw[:, s, q * 32:(q + 1) * 32, :],
                  tw_d[:, (s * 128 + q * 32) * 128:(s * 128 + q * 32 + 32) * 128].rearrange("p (k j) -> p k j", j=128),
                  [], [tw], tw)
    c256 = P.sbuf("c256_s", [128, 2, 2, 256], BF16)
    P.dma("sp", c256[:], c256_d[:].rearrange("p (a b k) -> p a b k", a=2, b=2), [], [c256], c256)
    zc = P.sbuf("zc_s", [128, 2, 2, 256], BF16)
    P.dma("sp", zc[:], zc_d[:].rearrange("p (a b k) -> p a b k", a=2, b=2), [], [zc], zc)

    Z = [P.sbuf(f"Z{i}", [128, 2, 128, MCH], BF16) for i in range(2)]
    A = P.sbuf("A_s", [128, 2, 128, MCH], BF16)
    R = P.sbuf("R_s", [128, 128, MCH], F32)
    rcs = P.sbuf("rc_s", [128, 2, 256], F32)
    ps = [P.psum(f"ps{i}", [128, 512], F32) for i in range(6)]
    pi = [0]

    def nxt():
        b = ps[pi[0] % len(ps)]
        pi[0] += 1
        return b

    for kh in range(2):
        pb = nxt()
        i = 0
        for nch in range(2):
            for sel in range(2):
                P.op("pe", lambda e: e.matmul(pb[:, :256], c256[:, nch, sel, kh * 128:(kh + 1) * 128], zc[:, nch, sel, :],
                                              start=(i == 0), stop=(i == 3)), [c256, zc], [pb], pe_acc=True)
                i += 1
        P.op("act", lambda e: e.activation(out=rcs[:, kh, :], in_=pb[:, :256], func=AF.Copy, scale=1.0 / 256.0),
             [pb], [rcs.sub(kh)])
    P.dma("sp", rc_d[:].rearrange("p (a k) -> p a k", a=2), rcs[:], [rcs], [], rcs, is_output=True)

    ev = 0
    for mc in range(NMC):
        Zt = Z[mc % 2]
        for part in range(2):
            P.dma("sp", Zt[:, part, :, :],
                  z_d[mc, :, part * 128 * MCH:(part + 1) * 128 * MCH].rearrange("p (n m) -> p n m", m=MCH),
                  [], [Zt.sub(part)], Zt, sem_key=part)
        for mi in range(MCH):
            if mi % 2 == 0:
                pb = nxt()
            off = (mi % 2) * 256
            for part in range(2):
                P.op("pe", lambda e: e.matmul(pb[:, off:off + 256], Zt[:, part, :, mi], dft[:, part, :],
                                              start=(part == 0), stop=(part == 1)),
                     [Zt.sub(part), dft], [pb], pe_acc=True)
            src = pb[:, off:off + 256].rearrange("p (r k) -> p r k", r=2)
            if (ev // 2) % 2 == 0:
                P.op("act", lambda e: e.copy(A[:, :, :, mi], src), [pb], [A.sub(mi)])
            else:
                P.op("dve", lambda e: e.tensor_copy(A[:, :, :, mi], src), [pb], [A.sub(mi)])
            ev += 1
        per = 512 // MCH
        for k1 in range(128):
            if k1 % per == 0:
                pb = nxt()
            off = (k1 % per) * MCH
            for sel in range(2):
                P.op("pe", lambda e: e.matmul(pb[:, off:off + MCH], tw[:, sel, k1, :], A[:, sel, k1, :],
                                              start=(sel == 0), stop=(sel == 1)),
                     [tw, A], [pb], pe_acc=True)
            if k1 % per == per - 1:
                k0 = k1 - per + 1
                dst = R[:, k0:k0 + per, :].rearrange("p k m -> p (k m)")
                if ev % 2 == 0:
                    P.op("act", lambda e: e.activation(out=dst, in_=pb[:, :], func=AF.Copy, scale=1.0 / 2048.0),
                         [pb], [R.sub(k0)])
                else:
                    P.op("dve", lambda e: e.tensor_scalar(dst, pb[:, :], 1.0 / 2048.0, None, ALU.mult), [pb], [R.sub(k0)])
                ev += 1
        P.dma("sp", r_d[mc, :, :], R[:].rearrange("p k m -> p (k m)"), [R], [], R, is_output=True)
    P.finish()
    P.close()
    return nc


def four_consts():
    import ml_dtypes
    bf = ml_dtypes.bfloat16
    n = np.arange(128)
    a = 2 * np.pi * np.outer(n, n) / 128.0
    C, S = np.cos(a), np.sin(a)
    dft = np.stack([np.concatenate([C, S], 1), np.concatenate([-S, C], 1)], 1)
    n2 = np.arange(128)[:, None, None]
    k1 = np.arange(128)[None, :, None]
    k2 = np.arange(128)[None, None, :]
    ang = 2 * np.pi * ((n2 * (k1 + 128 * k2)) % 16384) / 16384.0
    tw = np.stack([np.cos(ang), -np.sin(ang)], 1)
    nn = np.arange(256)
    a2 = 2 * np.pi * ((np.outer(nn, nn)) % 256) / 256.0
    c256 = np.stack([np.cos(a2), -np.sin(a2)], 1)
    c256 = c256.reshape(2, 128, 2, 256).transpose(1, 0, 2, 3)
    return dict(dft=np.ascontiguousarray(dft.reshape(128, 512)).astype(bf),
                tw=np.ascontiguousarray(tw.reshape(128, -1)).astype(bf),
                c256=np.ascontiguousarray(c256.reshape(128, -1)).astype(bf))


def four_layout_in(pq_lat, pq_ctx):
    z = pq_lat.reshape(128, 128, 2, NMC, MCH).transpose(3, 0, 2, 1, 4)
    zc = pq_ctx.reshape(2, 128, 2, 256).transpose(1, 0, 2, 3)
    return np.ascontiguousarray(z.reshape(NMC, 128, -1)), np.ascontiguousarray(zc.reshape(128, -1))


def four_layout_out(r, rc):
    f = r.reshape(NMC, 128, 128, MCH).transpose(1, 2, 0, 3).reshape(16384, 256)
    fc = rc.reshape(128, 2, 256).transpose(1, 0, 2).reshape(256, 256)
    return f, fc


NCOL = 4608


def build_mod():
    nc = bass.Bass("TRN2", target_bir_lowering=False)
    P = Prog(nc)
    w_d = P.dram("w", [1024, NCOL], F32, "ExternalInput")
    b_d = P.dram("b", [1, NCOL], F32, "ExternalInput")
    c_d = P.dram("c", [128, 8 * 3], F32, "ExternalInput")
    m_d = P.dram("m", [3, NCOL], F32, "ExternalOutput")
    cs = P.sbuf("cs", [128, 8, 3], F32)
    P.dma("sp", cs[:], c_d[:].rearrange("p (k r) -> p k r", r=3), [], [cs], cs)
    sc = P.sbuf("sc", [128, 8, 3], F32)
    P.op("act", lambda e: e.activation(out=sc[:], in_=cs[:], func=AF.Silu), [cs], [sc])
    bb = P.sbuf("bb", [3, NCOL], F32)
    P.dma("sp", bb[:], b_d[:, :].partition_broadcast(3), [], [bb], bb)
    W = [P.sbuf(f"W{i}", [128, 8, 512], F32) for i in range(2)]
    ps = [P.psum(f"ps{i}", [128, 512], F32) for i in range(2)]
    mo = P.sbuf("mo", [3, NCOL], F32)
    for j in range(NCOL // 512):
        Wt = W[j % 2]
        pb = ps[j % 2]
        for kc in range(8):
            P.dma("sp", Wt[:, kc, :], w_d[kc * 128:(kc + 1) * 128, j * 512:(j + 1) * 512], [], [Wt.sub(kc)], Wt, sem_key=kc)
        for kc in range(8):
            P.op("pe", lambda e: e.matmul(pb[:3, :], sc[:, kc, :], Wt[:, kc, :], start=(kc == 0), stop=(kc == 7)),
                 [sc, Wt.sub(kc)], [pb], pe_acc=True)
        P.op("dve", lambda e: e.tensor_tensor(mo[:, j * 512:(j + 1) * 512], pb[:3, :], bb[:, j * 512:(j + 1) * 512], ALU.add),
             [pb, bb], [mo.sub(j)])
    P.dma("sp", m_d[:, :], mo[:], [mo], [], mo, is_output=True)
    P.finish()
    P.close()
    return nc


NCTX = 256


def build_ret(NCH_LAT, pass2):
    nc = bass.Bass("TRN2", target_bir_lowering=False)
    P = Prog(nc)
    NLAT = 128 * NCH_LAT
    NTOK = NCTX + NLAT
    hT_d = P.dram("hT", [KC, 128, NTOK], BF16, "ExternalInput")
    win_d = P.dram("win", [1024, 6144], F32, "ExternalInput")
    dec_d = P.dram("dec", [1, 8], F32, "ExternalInput")
    ropec_d = P.dram("ropec", [128, NLAT], F32, "ExternalInput")
    ropes_d = P.dram("ropes", [128, NLAT], F32, "ExternalInput")
    cmask_d = P.dram("cmask", [128, 5 * 128], F32, "ExternalInput")
    qrow_d = P.dram("qrow", [2, 128], F32, "ExternalInput")
    kcol_d = P.dram("kcol", [128, 2], F32, "ExternalInput")
    cexp_d = P.dram("cexp", [1, NCH_LAT + 2], F32, "ExternalInput")
    comb_d = P.dram("comb", [1, 20], F32, "ExternalInput")
    ident_d = P.dram("ident", [128, 128], BF16, "ExternalInput")
    if pass2:
        ain_d = P.dram("ain", [4, 128, 8192], F32, "ExternalInput")
        o_d = P.dram("o", [NTOK, 2048], BF16, "ExternalOutput")
        spill_d = P.dram("spill", [4, NCH_LAT + 2, 128, 1024], BF16, "Internal")
    else:
        aout_d = P.dram("aout", [128, 8192], F32, "ExternalOutput")

    ident = P.sbuf("identb", [128, 128], BF16)
    P.dma("sp", ident[:], ident_d[:], [], [ident], ident)
    cmask = P.sbuf("cmask_s", [128, 5, 128], F32)
    P.dma("sp", cmask[:], cmask_d[:].rearrange("p (a b) -> p a b", a=5), [], [cmask], cmask)
    qrow = P.sbuf("qrow_s", [128, 2, 128], F32)
    for r in range(2):
        P.dma("sp", qrow[:, r, :], qrow_d[r:r + 1, :].partition_broadcast(128), [], [qrow], qrow)
    kcol = P.sbuf("kcol_s", [128, 2], F32)
    P.dma("sp", kcol[:], kcol_d[:], [], [kcol], kcol)
    NCE = NCH_LAT + 2
    cexp = P.sbuf("cexp_s", [128, NCE], F32)
    P.dma("sp", cexp[:], cexp_d[:, :].partition_broadcast(128), [], [cexp], cexp)
    comb = P.sbuf("comb_s", [128, 20], F32)
    P.dma("sp", comb[:], comb_d[:, :].partition_broadcast(128), [], [comb], comb)
    lg = P.sbuf("lg_s", [128, 8], F32)
    P.dma("sp", lg[:], dec_d[:, :].partition_broadcast(128), [], [lg], lg)
    lgn = P.sbuf("lgn_s", [128, 8], F32)
    P.op("dve", lambda e: e.tensor_scalar(lgn[:], lg[:], -1.0, None, ALU.mult), [lg], [lgn])
    P.op("dve", lambda e: e.tensor_tensor(lg[:], lg[:], lgn[:], ALU.min), [lg, lgn], [lg])
    c128 = P.sbuf("c128", [128, 1], F32)
    P.op("dve", lambda e: e.memset(c128[:], 128.0), [], [c128])

    DT = P.sbuf("DT", [128, 4, 128], F32)
    qd = P.sbuf("qd", [128, 4, 2, 128], F32)
    kd = P.sbuf("kd", [128, 4, 2], F32)
    g128 = P.sbuf("g128", [128, 4, 2], F32)
    wA = P.sbuf("wA", [128, 4, 2, NCE], F32)
    wc = P.sbuf("wc", [128, 4, 10], F32)
    tmpa = P.sbuf("tmpa", [128, 128], F32)
    tmpb = P.sbuf("tmpb", [128, 128], F32)
    for h in range(4):
        lf = lg[:, h:h + 1]
        lb = lg[:, 4 + h:5 + h]
        P.op("act", lambda e: e.activation(out=tmpa[:], in_=cmask[:, 0, :], func=AF.Exp, scale=lf), [cmask, lg], [tmpa])
        P.op("dve", lambda e: e.tensor_tensor(DT[:, h, :], tmpa[:], cmask[:, 2, :], ALU.mult), [tmpa, cmask], [DT.sub(h)])
        P.op("act", lambda e: e.activation(out=tmpb[:], in_=cmask[:, 1, :], func=AF.Exp, scale=lb), [cmask, lg], [tmpb])
        P.op("dve", lambda e: e.tensor_tensor(tmpb[:], tmpb[:], cmask[:, 3, :], ALU.mult), [tmpb, cmask], [tmpb])
        P.op("dve", lambda e: e.tensor_tensor(DT[:, h, :], DT[:, h, :], tmpb[:], ALU.add), [tmpb, DT.sub(h)], [DT.sub(h)])
        P.op("dve", lambda e: e.tensor_tensor(DT[:, h, :], DT[:, h, :], cmask[:, 4, :], ALU.add), [cmask, DT.sub(h)], [DT.sub(h)])
        P.op("act", lambda e: e.activation(out=qd[:, h, 0, :], in_=qrow[:, 0, :], func=AF.Exp, scale=lf), [qrow, lg], [qd.sub(h)])
        P.op("act", lambda e: e.activation(out=qd[:, h, 1, :], in_=qrow[:, 1, :], func=AF.Exp, scale=lb), [qrow, lg], [qd.sub(h)])
        P.op("act", lambda e: e.activation(out=kd[:, h, 0:1], in_=kcol[:, 0:1], func=AF.Exp, scale=lf), [kcol, lg], [kd.sub(h)])
        P.op("act", lambda e: e.activation(out=kd[:, h, 1:2], in_=kcol[:, 1:2], func=AF.Exp, scale=lb), [kcol, lg], [kd.sub(h)])
        P.op("act", lambda e: e.activation(out=g128[:, h, 0:1], in_=c128[:], func=AF.Exp, scale=lf), [c128, lg], [g128.sub(h)])
        P.op("act", lambda e: e.activation(out=g128[:, h, 1:2], in_=c128[:], func=AF.Exp, scale=lb), [c128, lg], [g128.sub(h)])
        P.op("act", lambda e: e.activation(out=wA[:, h, 0, :], in_=cexp[:], func=AF.Exp, scale=lf), [cexp, lg], [wA.sub(h)])
        P.op("act", lambda e: e.activation(out=wA[:, h, 1, :], in_=cexp[:], func=AF.Exp, scale=lb), [cexp, lg], [wA.sub(h)])
        P.op("act", lambda e: e.activation(out=wc[:, h, 0:5], in_=comb[:, 0:5], func=AF.Exp, scale=lf), [comb, lg], [wc.sub(h)])
        P.op("act", lambda e: e.activation(out=wc[:, h, 5:10], in_=comb[:, 5:10], func=AF.Exp, scale=lb), [comb, lg], [wc.sub(h)])
        P.op("dve", lambda e: e.tensor_tensor(wc[:, h, :], wc[:, h, :], comb[:, 10:20], ALU.mult), [comb, wc.sub(h)], [wc.sub(h)])

    win = P.sbuf("win_b", [128, KC, 6144], BF16)
    stages = [P.sbuf(f"stg{i}", [128, 512], F32) for i in range(4)]
    ctr = 0
    for kc in range(KC):
        for o in range(0, 6144, 512):
            st = stages[ctr % 4]
            P.dma("sp", st[:], win_d[kc * 128:(kc + 1) * 128, o:o + 512], [], [st], st)
            if 1024 <= o < 2048:
                P.op("dve", lambda e: e.tensor_scalar(win[:, kc, o:o + 512], st[:], 0.0625, None, ALU.mult), [st], [win.sub((kc, o))])
            else:
                k = ("act", "pool", "dve")[ctr % 3]
                if k == "act":
                    P.op("act", lambda e: e.copy(win[:, kc, o:o + 512], st[:]), [st], [win.sub((kc, o))])
                else:
                    P.op(k, lambda e: e.tensor_copy(win[:, kc, o:o + 512], st[:]), [st], [win.sub((kc, o))])
            ctr += 1

    hb = [P.sbuf(f"hb{i}", [128, KC, 512], BF16) for i in range(2)]
    rc = [P.sbuf(f"rc{i}", [128, 512], F32) for i in range(2)]
    rs = [P.sbuf(f"rs{i}", [128, 512], F32) for i in range(2)]
    ra = P.sbuf("ra", [128, 512], F32)
    rb = P.sbuf("rb", [128, 512], F32)
    kTr = P.sbuf("kTr", [128, 2, 512], BF16)
    qTr = P.sbuf("qTr", [128, 2, 512], BF16)
    vb = [P.sbuf(f"vb{i}", [128, 512], BF16) for i in range(2)]
    kf = P.sbuf("kf", [128, 256], BF16)
    kb = P.sbuf("kb", [128, 256], BF16)
    B = [P.psum(f"B{i}", [128, 512], F32) for i in range(7)]
    KTP = P.psum("KTP", [128, 1024], BF16)
    Sb = {s: P.sbuf(f"Sb_{s}", [128, 2, 512], F32) for s in ("ctx", "lat")}
    Af = {s: P.sbuf(f"Af_{s}", [128, 2, 512], F32) for s in ("ctx", "lat")}
    Sb_bf = P.sbuf("Sb_bf", [128, 2, 512], BF16)
    if pass2:
        Sf = P.sbuf("Sf", [128, 2, 512], F32)
        Sf_bf = P.sbuf("Sf_bf", [128, 2, 512], BF16)
        Sbin = P.sbuf("Sbin", [128, 2, 512], F32)
        astg = [P.sbuf(f"astg{i}", [128, 2, 512], F32) for i in range(2)]
        sbl = [P.sbuf(f"sbl{i}", [128, 2, 512], BF16) for i in range(2)]
        Sbt = P.sbuf("Sbt", [128, 2, 512], BF16)
        sg = P.sbuf("sg", [128, 512], F32)
        sT = P.sbuf("sT", [128, 128], BF16)
        qf = P.sbuf("qf", [128, 2, 128], BF16)
        qb = P.sbuf("qb", [128, 2, 128], BF16)
        stt = P.sbuf("stt", [128, 8], F32)
        junk = P.sbuf("junk", [128, 512], BF16)
        yn = P.sbuf("yn", [128, 512], F32)
        ob = [P.sbuf(f"ob{i}", [128, 512], BF16) for i in range(2)]

    bi = [0]
    ci = [0]

    def load_block(tok0, n, lat0, rope):
        H = hb[bi[0] % 2]
        P.dma("sp", H[:, :, :n], hT_d[:, :, tok0:tok0 + n].rearrange("k p t -> p k t"), [], [H], H)
        RC = RS = None
        if rope:
            RC = rc[bi[0] % 2]
            RS = rs[bi[0] % 2]
            P.dma("sp", RC[:, :n], ropec_d[:, lat0:lat0 + n], [], [RC], RC)
            P.dma("sp", RS[:, :n], ropes_d[:, lat0:lat0 + n], [], [RS], RS)
        bi[0] += 1
        return H, RC, RS

    def proj_T(H, n, col0, dst, RC, RS):
        for dc in range(2):
            for kc in range(KC):
                P.op("pe", lambda e: e.matmul(B[dc][:, :n], win[:, kc, col0 + dc * 128:col0 + (dc + 1) * 128], H[:, kc, :n],
                                              start=(kc == 0), stop=(kc == KC - 1)), [win, H], [B[dc]], pe_acc=True)
        if RC is None:
            P.op("act", lambda e: e.copy(dst[:, 0, :n], B[0][:, :n]), [B[0]], [dst.sub(0)])
            P.op("dve", lambda e: e.tensor_copy(dst[:, 1, :n], B[1][:, :n]), [B[1]], [dst.sub(1)])
            return
        P.op("dve", lambda e: e.tensor_tensor(ra[:, :n], B[0][:, :n], RC[:, :n], ALU.mult), [B[0], RC], [ra])
        P.op("dve", lambda e: e.tensor_tensor(rb[:, :n], B[1][:, :n], RS[:, :n], ALU.mult), [B[1], RS], [rb])
        P.op("pool", lambda e: e.tensor_tensor(dst[:, 0, :n], ra[:, :n], rb[:, :n], ALU.subtract), [ra, rb], [dst.sub(0)])
        P.op("dve", lambda e: e.tensor_tensor(ra[:, :n], B[1][:, :n], RC[:, :n], ALU.mult), [B[1], RC], [ra])
        P.op("dve", lambda e: e.tensor_tensor(rb[:, :n], B[0][:, :n], RS[:, :n], ALU.mult), [B[0], RS], [rb])
        P.op("pool", lambda e: e.tensor_tensor(dst[:, 1, :n], ra[:, :n], rb[:, :n], ALU.add), [ra, rb], [dst.sub(1)])

    def proj_v(H, j0, col0, bank):
        for kc in range(KC):
            P.op("pe", lambda e: e.matmul(bank[:, :], H[:, kc, j0:j0 + 128], win[:, kc, col0:col0 + 512],
                                          start=(kc == 0), stop=(kc == KC - 1)), [win, H], [bank], pe_acc=True)

    def k_tok(h, j0, need_b):
        for dc in range(2):
            P.op("pe", lambda e: e.transpose(KTP[:, dc * 128:(dc + 1) * 128], kTr[:, dc, j0:j0 + 128], ident[:]),
                 [kTr, ident], [KTP], pe_acc=True)
        P.op("act", lambda e: e.activation(out=kf[:], in_=KTP[:, :256], func=AF.Copy, scale=kd[:, h, 0:1]), [KTP, kd], [kf])
        if need_b:
            P.op("act", lambda e: e.activation(out=kb[:], in_=KTP[:, :256], func=AF.Copy, scale=kd[:, h, 1:2]), [KTP, kd], [kb])

    for h in range(4):
        cq, ck, cv, cg = h * 256, 1024 + h * 256, 2048 + h * 512, 4096 + h * 512
        segs = [("lat", NCTX, NCH_LAT, True, 2)]
        if pass2:
            segs = [("ctx", 0, 2, False, 0)] + segs
        for (sname, tbase, nch, rope, ce0) in segs:
            S_b, A_f = Sb[sname], Af[sname]
            P.op("dve", lambda e: e.memset(S_b[:], 0.0), [], [S_b])
            P.op("pool", lambda e: e.memset(A_f[:], 0.0), [], [A_f])
            P.op("pool", lambda e: e.memset(Sb_bf[:], 0.0), [], [Sb_bf])
            blocks = [(c0, min(4, nch - c0)) for c0 in range(0, nch, 4)]
            for (c0, nb) in reversed(blocks):
                n = nb * 128
                H, RC, RS = load_block(tbase + c0 * 128, n, c0 * 128, rope)
                proj_T(H, n, ck, kTr, RC, RS)
                for c in reversed(range(c0, c0 + nb)):
                    j0 = (c - c0) * 128
                    V = vb[ci[0] % 2]
                    ci[0] += 1
                    proj_v(H, j0, cv, B[2])
                    P.op("act", lambda e: e.copy(V[:], B[2][:, :]), [B[2]], [V])
                    k_tok(h, j0, True)
                    for dc in range(2):
                        P.op("pe", lambda e: e.matmul(B[3 + dc][:, :], kb[:, dc * 128:(dc + 1) * 128], V[:], start=True, stop=True),
                             [kb, V], [B[3 + dc]], pe_acc=True)
                        P.op("pe", lambda e: e.matmul(B[5 + dc][:, :], kf[:, dc * 128:(dc + 1) * 128], V[:], start=True, stop=True),
                             [kf, V], [B[5 + dc]], pe_acc=True)
                    if pass2:
                        P.dma("sp", spill_d[h, ce0 + c, :, :], Sb_bf[:].rearrange("p a b -> p (a b)"), [Sb_bf],
                              [spill_d.sub((h, ce0 + c))], Sb_bf)
                    for dc in range(2):
                        P.op("dve", lambda e: e.scalar_tensor_tensor(S_b[:, dc, :], S_b[:, dc, :], g128[:, h, 1:2], B[3 + dc][:, :],
                                                                     ALU.mult, ALU.add), [S_b, g128, B[3 + dc]], [S_b])
                        P.op("dve", lambda e: e.scalar_tensor_tensor(A_f[:, dc, :], B[5 + dc][:, :],
                                                                     wA[:, h, 0, ce0 + (nch - 1 - c):ce0 + (nch - 1 - c) + 1] if False else wA[:, h, 0, ce0 + c:ce0 + c + 1],
                                                                     A_f[:, dc, :], ALU.mult, ALU.add), [A_f, wA, B[5 + dc]], [A_f])
                    if pass2:
                        P.op("act", lambda e: e.copy(Sb_bf[:], S_b[:]), [S_b], [Sb_bf])
            if not pass2:
                base = (h * 2 + 0) * 1024
                P.dma("sp", aout_d[:, base:base + 1024], A_f[:].rearrange("p a b -> p (a b)"), [A_f], [], A_f, is_output=True)
                base = (h * 2 + 1) * 1024
                P.dma("sp", aout_d[:, base:base + 1024], S_b[:].rearrange("p a b -> p (a b)"), [S_b], [], S_b, is_output=True)
        if not pass2:
            continue
        for (sname, tbase, nch, rope, ce0) in segs:
            if sname == "ctx":
                P.op("dve", lambda e: e.memset(Sf[:], 0.0), [], [Sf])
                P.op("pool", lambda e: e.memset(Sf_bf[:], 0.0), [], [Sf_bf])
            else:
                for (dst, src_ctx, d, wo) in ((Sf, Af["ctx"], 0, 0), (Sbin, Sb["ctx"], 1, 5)):
                    P.op("dve", lambda e: e.tensor_scalar(dst[:], src_ctx[:], wc[:, h, wo + 4:wo + 5], None, ALU.mult),
                         [src_ctx, wc], [dst])
                    for s in range(4):
                        ag = astg[s % 2]
                        base = (h * 2 + d) * 1024
                        P.dma("sp", ag[:].rearrange("p a b -> p (a b)"), ain_d[s, :, base:base + 1024], [], [ag], ag)
                        P.op("dve", lambda e: e.scalar_tensor_tensor(dst[:], ag[:], wc[:, h, wo + s:wo + s + 1], dst[:],
                                                                     ALU.mult, ALU.add), [ag, wc, dst], [dst])
                P.op("act", lambda e: e.copy(Sf_bf[:], Sf[:]), [Sf], [Sf_bf])
            blocks = [(c0, min(4, nch - c0)) for c0 in range(0, nch, 4)]
            for (c0, nb) in blocks:
                n = nb * 128
                H, RC, RS = load_block(tbase + c0 * 128, n, c0 * 128, rope)
                proj_T(H, n, cq, qTr, RC, RS)
                proj_T(H, n, ck, kTr, RC, RS)
                for c in range(c0, c0 + nb):
                    j0 = (c - c0) * 128
                    V = vb[ci[0] % 2]
                    O = ob[ci[0] % 2]
                    SL = sbl[ci[0] % 2]
                    ci[0] += 1
                    P.dma("sp", SL[:].rearrange("p a b -> p (a b)"), spill_d[h, ce0 + c, :, :], [spill_d.sub((h, ce0 + c))], [SL], SL)
                    proj_v(H, j0, cv, B[2])
                    P.op("act", lambda e: e.copy(V[:], B[2][:, :]), [B[2]], [V])
                    proj_v(H, j0, cg, B[3])
                    P.op("act", lambda e: e.activation(out=sg[:], in_=B[3][:, :], func=AF.Silu), [B[3]], [sg])
                    k_tok(h, j0, False)
                    for dc in range(2):
                        P.op("pe", lambda e: e.matmul(B[4][:, :128], kTr[:, dc, j0:j0 + 128], qTr[:, dc, j0:j0 + 128],
                                                      start=(dc == 0), stop=(dc == 1)), [kTr, qTr], [B[4]], pe_acc=True)
                    P.op("dve", lambda e: e.tensor_tensor(sT[:], B[4][:, :128], DT[:, h, :], ALU.mult), [B[4], DT], [sT])
                    P.op("pool", lambda e: e.tensor_tensor(qf[:], qTr[:, :, j0:j0 + 128],
                                                           qd[:, h, 0:1, :].to_broadcast([128, 2, 128]), ALU.mult), [qTr, qd], [qf])
                    P.op("pool", lambda e: e.tensor_tensor(qb[:], qTr[:, :, j0:j0 + 128],
                                                           qd[:, h, 1:2, :].to_broadcast([128, 2, 128]), ALU.mult), [qTr, qd], [qb])
                    if sname == "ctx":
                        SBT = SL
                    else:
                        SBT = Sbt
                        P.op("dve", lambda e: e.scalar_tensor_tensor(Sbt[:], Sbin[:], wA[:, h, 1, ce0 + c:ce0 + c + 1], SL[:],
                                                                     ALU.mult, ALU.add), [Sbin, wA, SL], [Sbt])
                    P.op("pe", lambda e: e.matmul(B[5][:, :], sT[:], V[:], start=True, stop=False), [sT, V], [B[5]], pe_acc=True)
                    for dc in range(2):
                        P.op("pe", lambda e: e.matmul(B[5][:, :], qf[:, dc, :], Sf_bf[:, dc, :], start=False, stop=False),
                             [qf, Sf_bf], [B[5]], pe_acc=True)
                    for dc in range(2):
                        P.op("pe", lambda e: e.matmul(B[5][:, :], qb[:, dc, :], SBT[:, dc, :], start=False, stop=(dc == 1)),
                             [qb, SBT], [B[5]], pe_acc=True)
                    P.op("act", lambda e: e.activation(out=junk[:], in_=B[5][:, :], func=AF.Identity, accum_out=stt[:, 0:1]),
                         [B[5]], [junk, stt])
                    P.op("act", lambda e: e.activation(out=junk[:], in_=B[5][:, :], func=AF.Square, accum_out=stt[:, 1:2]),
                         [B[5]], [junk, stt])
                    P.op("dve", lambda e: e.tensor_scalar(stt[:, 2:3], stt[:, 0:1], 1.0 / 512, None, ALU.mult), [stt], [stt])
                    P.op("dve", lambda e: e.tensor_tensor(stt[:, 3:4], stt[:, 2:3], stt[:, 2:3], ALU.mult), [stt], [stt])
                    P.op("dve", lambda e: e.scalar_tensor_tensor(stt[:, 4:5], stt[:, 1:2], 1.0 / 512, stt[:, 3:4], ALU.mult,
                                                                 ALU.subtract), [stt], [stt])
                    P.op("dve", lambda e: e.tensor_scalar(stt[:, 4:5], stt[:, 4:5], EPS, None, ALU.add), [stt], [stt])
                    P.op("act", lambda e: e.activation(out=stt[:, 5:6], in_=stt[:, 4:5], func=AF.Sqrt), [stt], [stt])
                    P.op("dve", lambda e: e.reciprocal(stt[:, 6:7], stt[:, 5:6]), [stt], [stt])
                    P.op("dve", lambda e: e.scalar_tensor_tensor(stt[:, 7:8], stt[:, 2:3], -1.0, stt[:, 6:7], ALU.mult, ALU.mult),
                         [stt], [stt])
                    P.op("act", lambda e: e.activation(out=yn[:], in_=B[5][:, :], func=AF.Identity, scale=stt[:, 6:7],
                                                        bias=stt[:, 7:8]), [B[5], stt], [yn])
                    P.op("pool", lambda e: e.tensor_tensor(O[:], yn[:], sg[:], ALU.mult), [yn, sg], [O])
                    t0 = tbase + c * 128
                    P.dma("sp", o_d[t0:t0 + 128, h * 512:(h + 1) * 512], O[:], [O], [], O, is_output=True)
                    P.op("pe", lambda e: e.matmul(B[6][:, :], kf[:, 0:128], V[:], start=True, stop=True), [kf, V], [B[6]], pe_acc=True)
                    P.op("pe", lambda e: e.matmul(B[4][:, :], kf[:, 128:256], V[:], start=True, stop=True), [kf, V], [B[4]], pe_acc=True)
                    for dc, bk in ((0, B[6]), (1, B[4])):
                        P.op("dve", lambda e: e.scalar_tensor_tensor(Sf[:, dc, :], Sf[:, dc, :], g128[:, h, 0:1], bk[:, :],
                                                                     ALU.mult, ALU.add), [Sf, g128, bk], [Sf])
                    P.op("act", lambda e: e.copy(Sf_bf[:], Sf[:]), [Sf], [Sf_bf])
    P.finish()
    P.close()
    return nc


def ret_consts(NCH_LAT, seg_j, lat_pos0):
    j = np.arange(128)[:, None]
    i = np.arange(128)[None, :]
    posU = np.maximum(i - j, 0).astype(np.float32)
    posL = np.maximum(j - i, 0).astype(np.float32)
    U = (i > j).astype(np.float32)
    L = (i < j).astype(np.float32)
    I2 = 2.0 * (i == j).astype(np.float32)
    cmask = np.stack([posU, posL, U, L, I2], 1).reshape(128, 5 * 128)
    qrow = np.stack([np.arange(128) + 1.0, 128.0 - np.arange(128)]).astype(np.float32)
    kcol = np.stack([127.0 - np.arange(128), np.arange(128) * 1.0], 1).astype(np.float32)
    cexp = np.concatenate([128.0 * (1 - np.arange(2)), 128.0 * (NCH_LAT - 1 - np.arange(NCH_LAT))])[None, :].astype(np.float32)
    Lseg = 128.0 * NCH_LAT
    ex = np.zeros(10, np.float32)
    mk = np.zeros(10, np.float32)
    for s in range(4):
        if s < seg_j:
            ex[s] = Lseg * (seg_j - 1 - s); mk[s] = 1
        if s > seg_j:
            ex[5 + s] = Lseg * (s - seg_j - 1); mk[5 + s] = 1
    ex[4] = Lseg * seg_j; mk[4] = 1
    ex[9] = Lseg * (3 - seg_j); mk[9] = 1
    comb = np.concatenate([ex, mk])[None, :].astype(np.float32)
    n = np.arange(lat_pos0, lat_pos0 + 128 * NCH_LAT)
    row = (n // 64).astype(np.float32)
    col = (n % 64).astype(np.float32)
    inv = (np.float32(10000.0) ** (-(np.arange(64, dtype=np.float32)) / np.float32(64))).astype(np.float32)
    ang = np.concatenate([row[:, None] * inv, col[:, None] * inv], -1).astype(np.float32)
    ropec = np.ascontiguousarray(np.cos(ang).T).astype(np.float32)
    ropes = np.ascontiguousarray(np.sin(ang).T).astype(np.float32)
    return dict(cmask=cmask, qrow=qrow, kcol=kcol, cexp=cexp, comb=comb, ropec=ropec, ropes=ropes)


NCORES = 8
SEQ = 16384
NLATC = 4096
NCTXC = 64
NTC = NLATC + NCTXC
_BF = ml_dtypes.bfloat16
_PROGS = {}


def _prog(key, fn):
    if key not in _PROGS:
        _PROGS[key] = fn()
    return _PROGS[key]


def _run(nc, in_maps):
    res = run_bass_kernel_spmd(nc, in_maps, core_ids=list(range(NCORES)))
    return res.results


def _ppl(v):
    return np.ascontiguousarray(np.asarray(v, np.float32).reshape(8, 128).T)


def kernel(x, c, ctx, c_ctx, ada_w, ada_b, norm_g, final_g, ffn_w_gate, ffn_w_up, ffn_w_down,
           four_w, four_b, ret_w_in, ret_w_out, ret_decay):
    f32 = np.float32
    x = np.asarray(x, f32); ctx = np.asarray(ctx, f32)
    ident = np.eye(128).astype(_BF)
    zrow = np.zeros(1024, f32)
    X = []
    for i in range(NCORES):
        b, j = divmod(i, 4)
        X.append(np.ascontiguousarray(np.concatenate([x[b, j * NLATC:(j + 1) * NLATC], ctx[b, j * NCTXC:(j + 1) * NCTXC]], 0)))
    W = np.ascontiguousarray(np.asarray(ada_w, f32).transpose(1, 0, 2).reshape(1024, 4 * 9216))
    Bv = np.asarray(ada_b, f32).reshape(1, 4 * 9216)
    cin = np.stack([np.asarray(c, f32)[0], np.asarray(c, f32)[1], np.asarray(c_ctx, f32)])
    cl = np.ascontiguousarray(cin.reshape(3, 8, 128).transpose(2, 1, 0).reshape(128, 24))
    ncm = _prog("mod", build_mod)
    rm = _run(ncm, [dict(w=np.ascontiguousarray(W[:, i * NCOL:(i + 1) * NCOL]), b=np.ascontiguousarray(Bv[:, i * NCOL:(i + 1) * NCOL]), c=cl)
                    for i in range(NCORES)])
    M = np.concatenate([r["m"] for r in rm], 1).reshape(3, 4, 9, 1024)
    del W

    def tok_inputs(l, i, g_ffn, ffn_idx, g_out, out_idx, extra):
        b = i // 4
        ml, mc = M[b, l], M[2, l]
        z = zrow
        vecs = [g_ffn if g_ffn is not None else z]
        vecs += [ml[ffn_idx], ml[ffn_idx + 1], mc[ffn_idx], mc[ffn_idx + 1]] if ffn_idx is not None else [z, z, z, z]
        vecs += [g_out if g_out is not None else z]
        vecs += [ml[out_idx], ml[out_idx + 1], mc[out_idx], mc[out_idx + 1]] if out_idx is not None else [z, z, z, z]
        pp = np.ascontiguousarray(np.stack([_ppl(v) for v in vecs], 1))
        rows = np.stack([ml[ffn_idx + 2] if ffn_idx is not None else z, mc[ffn_idx + 2] if ffn_idx is not None else z,
                         ml[5], mc[5], extra.get("bmix", z), np.asarray(final_g, f32)]).astype(f32)
        d = dict(x=X[i], ident=ident, pp=pp, rows=np.ascontiguousarray(rows))
        return d

    cs = None
    fconst = None
    for l in range(4):
        mixer = l % 2
        slot = l // 2
        mode = "pq" if mixer == 0 else "hT"
        nca = _prog("tokA_" + mode, lambda: build_tok(NLATC, NCTXC, ffn=True, mix_k=0, norm_out=mode))
        wg = np.ascontiguousarray(np.asarray(ffn_w_gate, f32)[l, 0]); wu = np.ascontiguousarray(np.asarray(ffn_w_up, f32)[l, 0])
        wd = np.ascontiguousarray(np.asarray(ffn_w_down, f32)[l, 0])
        ins = []
        for i in range(NCORES):
            d = tok_inputs(l, i, np.asarray(norm_g, f32)[l, 0], 0, np.asarray(norm_g, f32)[l, 1], 3, {})
            d.update(wg=wg, wu=wu, wd=wd)
            if mode == "pq":
                if cs is None:
                    mm = np.arange(256)
                    ang = 2 * np.pi * ((np.outer(mm, mm)) % 256) / 256.0
                    csf = np.concatenate([np.cos(ang), np.sin(ang)], 1).reshape(2, 128, 512).transpose(1, 0, 2)
                    cs = np.ascontiguousarray(csf).astype(_BF)
                d["cs"] = cs
            ins.append(d)
        ra = _run(nca, ins)
        X = [r["y"] for r in ra]
        if mixer == 0:
            if fconst is None:
                fconst = four_consts()
            ncf = _prog("four", build_four)
            ins = []
            for i in range(NCORES):
                b, g = divmod(i, 4)
                lat = np.concatenate([ra[4 * b + j]["pq"][:NLATC, g, :] for j in range(4)], 0)
                cx = np.concatenate([ra[4 * b + j]["pq"][NLATC:, g, :] for j in range(4)], 0)
                z_, zc_ = four_layout_in(lat, cx)
                ins.append(dict(z=z_, zc=zc_, **fconst))
            rf = _run(ncf, ins)
            U = []
            fl = {}
            for i in range(NCORES):
                fl[i] = four_layout_out(rf[i]["r"], rf[i]["rc"])
            for i in range(NCORES):
                b, j = divmod(i, 4)
                lat = np.concatenate([fl[4 * b + g][0][j * NLATC:(j + 1) * NLATC] for g in range(4)], 1)
                cx = np.concatenate([fl[4 * b + g][1][j * NCTXC:(j + 1) * NCTXC] for g in range(4)], 1)
                U.append(np.ascontiguousarray(np.concatenate([lat, cx], 0).astype(f32)))
            mk = 1024
            wm = np.ascontiguousarray(np.asarray(four_w, f32)[slot])
            bmix = np.asarray(four_b, f32)[slot]
        else:
            nr1 = _prog("ret1", lambda: build_ret(NLATC // 128, False))
            nr2 = _prog("ret2", lambda: build_ret(NLATC // 128, True))
            win = np.ascontiguousarray(np.asarray(ret_w_in, f32)[slot])
            dec = np.ascontiguousarray(np.asarray(ret_decay, f32)[slot].reshape(1, 8))
            ins = []
            for i in range(NCORES):
                b, j = divmod(i, 4)
                hc = np.concatenate([ra[4 * b + jj]["hT"][:, :, NLATC:] for jj in range(4)], 2)
                hT = np.ascontiguousarray(np.concatenate([hc, ra[i]["hT"][:, :, :NLATC]], 2))
                d = dict(hT=hT, win=win, dec=dec, ident=ident)
                d.update(ret_consts(NLATC // 128, j, j * NLATC))
                ins.append(d)
            r1 = _run(nr1, ins)
            for i in range(NCORES):
                b = i // 4
                ins[i]["ain"] = np.ascontiguousarray(np.stack([r1[4 * b + s]["aout"] for s in range(4)]))
            r2 = _run(nr2, ins)
            U = []
            for i in range(NCORES):
                j = i % 4
                o = r2[i]["o"]
                U.append(np.ascontiguousarray(np.concatenate([o[256:], o[j * NCTXC:(j + 1) * NCTXC]], 0)))
            mk = 2048
            wm = np.ascontiguousarray(np.asarray(ret_w_out, f32)[slot])
            bmix = zrow
        ncb = _prog("tokB_%d" % mk, lambda: build_tok(NLATC, NCTXC, ffn=False, mix_k=mk, norm_out=None))
        ins = []
        for i in range(NCORES):
            d = tok_inputs(l, i, None, None, None, None, dict(bmix=bmix))
            d.update(u=U[i], wm=wm)
            ins.append(d)
        rb = _run(ncb, ins)
        X = [r["y"] for r in rb]
        last = (l == 3)
        ncc = _prog("tokC_final" if last else "tokC",
                    lambda: build_tok(NLATC, NCTXC, ffn=True, mix_k=0, norm_out=("final" if last else None)))
        wg = np.ascontiguousarray(np.asarray(ffn_w_gate, f32)[l, 1]); wu = np.ascontiguousarray(np.asarray(ffn_w_up, f32)[l, 1])
        wd = np.ascontiguousarray(np.asarray(ffn_w_down, f32)[l, 1])
        ins = []
        for i in range(NCORES):
            d = tok_inputs(l, i, np.asarray(norm_g, f32)[l, 2], 6, None, None, {})
            d.update(wg=wg, wu=wu, wd=wd)
            ins.append(d)
        rc_ = _run(ncc, ins)
        if last:
            rz = rc_
        else:
            X = [r["y"] for r in rc_]
    out = np.empty((2, SEQ, 1024), f32)
    for i in range(NCORES):
        b, j = divmod(i, 4)
        out[b, j * NLATC:(j + 1) * NLATC] = rz[i]["yf"][:NLATC]
    return out
```
